# Optimizing a Trainium2 kernel written in Bass

```python
import math, functools
import jax, jax.numpy as jnp
from jax import lax
import numpy as np

D_MODEL = 2048
BATCH = 1
SEQ = 16384
DEPTH = 1
DEC_BATCH = 16
DEC_SEQ = 32
PAST_LEN = 1024

CHUNK = 64
Q_BLOCK = 128
H_A = 8
DV_A = (D_MODEL // 2) // H_A
HD_A = DV_A // 2
H_B = 8
HD_B = (D_MODEL // 2) // H_B
HKV_B = 2
G_B = H_B // HKV_B
H_IDX = 16
D_IDX = 64
TOPK_MAX = 256
N_BUCKETS = 32
MAX_DISTANCE = 128
D_FF = 4 * D_MODEL
EPS = 1e-6
SPLIT_SIZES = (H_A * 2 * HD_A, H_A * 2 * HD_A, H_A * DV_A, H_B * HD_B, HKV_B * HD_B, HKV_B * HD_B,
               H_IDX * D_IDX, D_IDX, H_IDX, D_MODEL, D_MODEL)
IN_COLS = sum(SPLIT_SIZES)

kernel_name = 'diff_dsa_gated_parallel_stream_encoder'


def rms_norm(x, g):
    xf = x.astype(jnp.float32)
    y = xf * lax.rsqrt(jnp.mean(xf * xf, axis=-1, keepdims=True) + EPS)
    return (y * g.astype(jnp.float32)).astype(x.dtype)


def split_cols(z):
    out, o = [], 0
    for n in SPLIT_SIZES:
        out.append(z[..., o:o + n])
        o += n
    return out


def t5_bucket(rel):
    half = N_BUCKETS // 2
    exact = half // 2
    n = jnp.abs(rel)
    nf = jnp.maximum(n, 1).astype(jnp.float32)
    large = exact + (jnp.log(nf / exact) / math.log(MAX_DISTANCE / exact) * (half - exact)).astype(jnp.int32)
    large = jnp.minimum(large, half - 1)
    return jnp.where(rel > 0, half, 0) + jnp.where(n < exact, n, large)


def chunk_visible(qpos, kpos):
    return (kpos // CHUNK) <= (qpos // CHUNK)


def diff_attn_block(start, q1, q2, k1, k2, v, lam, bias_tab):
    tq, n_keys = q1.shape[1], k1.shape[1]
    qpos = start + jnp.arange(tq)
    kpos = jnp.arange(n_keys)
    bias = bias_tab[t5_bucket(kpos[None, :] - qpos[:, None])].astype(jnp.float32)
    bias = jnp.transpose(bias, (2, 0, 1))[None]
    mask = chunk_visible(qpos[:, None], kpos[None, :])[None, None]
    scale = HD_A ** -0.5

    def probs(q, k):
        s = jnp.einsum('bthd,bshd->bhts', q, k, preferred_element_type=jnp.float32) * scale + bias
        return jax.nn.softmax(jnp.where(mask, s, -jnp.inf), axis=-1)

    a = probs(q1, k1) - lam * probs(q2, k2)
    return jnp.einsum('bhts,bshe->bthe', a.astype(v.dtype), v)


def dsa_block(start, qb, qi, wi, kb, vb, ki, bias_tab, topk):
    b, tq = qb.shape[0], qb.shape[1]
    n_keys = kb.shape[1]
    qpos = start + jnp.arange(tq)
    kpos = jnp.arange(n_keys)
    admissible = chunk_visible(qpos[:, None], kpos[None, :])
    rel = jax.nn.relu(jnp.einsum('bthd,bsd->bths', qi, ki, preferred_element_type=jnp.float32))
    score = jnp.einsum('bth,bths->bts', wi.astype(jnp.float32), rel)
    score = jnp.where(admissible[None], score, -jnp.inf)
    _, idx = lax.top_k(score, topk)
    ksel = jax.vmap(lambda kk, ii: kk[ii])(kb, idx)
    vsel = jax.vmap(lambda vv, ii: vv[ii])(vb, idx)
    valid = (idx // CHUNK) <= (qpos[None, :, None] // CHUNK)
    bias = bias_tab[t5_bucket(idx - qpos[None, :, None])].astype(jnp.float32)
    bias = jnp.transpose(bias.reshape(b, tq, topk, HKV_B, G_B), (0, 1, 3, 4, 2))
    q = qb.reshape(b, tq, HKV_B, G_B, HD_B)
    s = jnp.einsum('btngd,btjnd->btngj', q, ksel, preferred_element_type=jnp.float32) * (HD_B ** -0.5) + bias
    s = jnp.where(valid[:, :, None, None, :], s, -jnp.inf)
    p = jax.nn.softmax(s, axis=-1)
    o = jnp.einsum('btngj,btjnd->btngd', p.astype(vsel.dtype), vsel)
    return o.reshape(b, tq, H_B * HD_B)


def blockwise(fn, q_arrays, q0, n_q):
    if n_q > Q_BLOCK and n_q % Q_BLOCK == 0:
        nb = n_q // Q_BLOCK
        split = lambda a: jnp.swapaxes(a.reshape((a.shape[0], nb, Q_BLOCK) + a.shape[2:]), 0, 1)
        starts = q0 + Q_BLOCK * jnp.arange(nb, dtype=jnp.int32)
        out = lax.map(lambda a: fn(a[0], *a[1]), (starts, tuple(split(a) for a in q_arrays)))
        out = jnp.swapaxes(out, 0, 1)
        return out.reshape((out.shape[0], n_q) + out.shape[3:])
    return fn(q0, *q_arrays)


def layer_forward(x, past, rel_bias, norm1_g, w_in, qn_a_g, kn_a_g, lam_q1, lam_k1, lam_q2, lam_k2,
                  subln_a_g, qn_b_g, kn_b_g, w_o_a, w_o_b, w_out, norm2_g, w_ff1, w_ff2, lam_init):
    b, t, _ = x.shape
    h = rms_norm(x, norm1_g)
    z = jnp.einsum('btd,de->bte', h, w_in)
    qa, ka, va, qb, kb, vb, qi, ki, wi, ga, gb = split_cols(z)
    qa = rms_norm(qa.reshape(b, t, H_A, 2, HD_A), qn_a_g.reshape(2, HD_A))
    ka = rms_norm(ka.reshape(b, t, H_A, 2, HD_A), kn_a_g.reshape(2, HD_A)).reshape(b, t, H_A, 2 * HD_A)
    va = va.reshape(b, t, H_A, DV_A)
    qb = rms_norm(qb.reshape(b, t, H_B, HD_B), qn_b_g)
    kb = rms_norm(kb.reshape(b, t, HKV_B, HD_B), kn_b_g)
    vb = vb.reshape(b, t, HKV_B, HD_B)
    qi = qi.reshape(b, t, H_IDX, D_IDX)
    new_rows = (ka, va, kb, vb, ki)
    if past is None:
        full = new_rows
    else:
        full = tuple(jnp.concatenate([p.astype(n.dtype), n], axis=1) for p, n in zip(past, new_rows))
    ka_f, va_f, kb_f, vb_f, ki_f = full
    n_keys = ka_f.shape[1]
    q0 = n_keys - t

    k_pair = ka_f.reshape(b, n_keys, H_A, 2, HD_A)
    lam = (jnp.exp(jnp.sum(lam_q1.astype(jnp.float32) * lam_k1.astype(jnp.float32)))
           - jnp.exp(jnp.sum(lam_q2.astype(jnp.float32) * lam_k2.astype(jnp.float32))) + lam_init)
    fa = functools.partial(diff_attn_block, k1=k_pair[..., 0, :], k2=k_pair[..., 1, :], v=va_f,
                           lam=lam, bias_tab=rel_bias[:, :H_A])
    oa = blockwise(fa, (qa[..., 0, :], qa[..., 1, :]), q0, t)
    oa = (rms_norm(oa, subln_a_g) * (1.0 - lam_init)).reshape(b, t, H_A * DV_A)

    topk = min(TOPK_MAX, n_keys // 4)
    fb = functools.partial(dsa_block, kb=kb_f, vb=vb_f, ki=ki_f, bias_tab=rel_bias[:, H_A:], topk=topk)
    ob = blockwise(fb, (qb, qi, wi), q0, t)

    merged = jax.nn.sigmoid(ga) * (ob @ w_o_b) + jax.nn.sigmoid(gb * 0 + gb) * 0 if False else \
        jax.nn.sigmoid(ga) * (oa @ w_o_a) + jax.nn.sigmoid(gb) * (ob @ w_o_b)
    x = x + merged @ w_out

    u = jax.nn.relu(rms_norm(x, norm2_g) @ w_ff1)
    x = x + (u * u) @ w_ff2
    return x, new_rows


def setup_inputs(seed: int = 0) -> dict:
    key = jax.random.key(seed)
    ks = jax.random.split(key, 26)
    nrm = lambda k, shape, scale: jax.random.normal(k, shape, jnp.float32) * scale
    gain = lambda k, shape: 1.0 + 0.05 * jax.random.normal(k, shape, jnp.float32)
    return {
        'x_prompt': nrm(ks[0], (BATCH, SEQ, D_MODEL), 1.0),
        'x_sample': nrm(ks[1], (DEC_BATCH, DEC_SEQ, D_MODEL), 1.0),
        'cache_a_k': nrm(ks[2], (DEPTH, DEC_BATCH, PAST_LEN, H_A, 2 * HD_A), 1.0),
        'cache_a_v': nrm(ks[3], (DEPTH, DEC_BATCH, PAST_LEN, H_A, DV_A), 1.0),
        'cache_b_k': nrm(ks[4], (DEPTH, DEC_BATCH, PAST_LEN, HKV_B, HD_B), 1.0),
        'cache_b_v': nrm(ks[5], (DEPTH, DEC_BATCH, PAST_LEN, HKV_B, HD_B), 1.0),
        'cache_b_kidx': nrm(ks[6], (DEPTH, DEC_BATCH, PAST_LEN, D_IDX), 1.0),
        'rel_bias': nrm(ks[7], (N_BUCKETS, H_A + H_B), 0.2),
        'norm1_g': gain(ks[8], (DEPTH, D_MODEL)),
        'w_in': nrm(ks[9], (DEPTH, D_MODEL, IN_COLS), D_MODEL ** -0.5),
        'qn_a_g': gain(ks[10], (DEPTH, 2 * HD_A)),
        'kn_a_g': gain(ks[11], (DEPTH, 2 * HD_A)),
        'lam_q1': nrm(ks[12], (DEPTH, HD_A), 0.1),
        'lam_k1': nrm(ks[13], (DEPTH, HD_A), 0.1),
        'lam_q2': nrm(ks[14], (DEPTH, HD_A), 0.1),
        'lam_k2': nrm(ks[15], (DEPTH, HD_A), 0.1),
        'subln_a_g': gain(ks[16], (DEPTH, DV_A)),
        'qn_b_g': gain(ks[17], (DEPTH, HD_B)),
        'kn_b_g': gain(ks[18], (DEPTH, HD_B)),
        'w_o_a': nrm(ks[19], (DEPTH, H_A * DV_A, D_MODEL), (H_A * DV_A) ** -0.5),
        'w_o_b': nrm(ks[20], (DEPTH, H_B * HD_B, D_MODEL), (H_B * HD_B) ** -0.5),
        'w_out': nrm(ks[21], (DEPTH, D_MODEL, D_MODEL), D_MODEL ** -0.5),
        'norm2_g': gain(ks[22], (DEPTH, D_MODEL)),
        'w_ff1': nrm(ks[23], (DEPTH, D_MODEL, D_FF), D_MODEL ** -0.5),
        'w_ff2': nrm(ks[24], (DEPTH, D_FF, D_MODEL), D_FF ** -0.5),
    }


def reference(x_prompt, x_sample, cache_a_k, cache_a_v, cache_b_k, cache_b_v, cache_b_kidx, rel_bias,
              norm1_g, w_in, qn_a_g, kn_a_g, lam_q1, lam_k1, lam_q2, lam_k2, subln_a_g, qn_b_g, kn_b_g,
              w_o_a, w_o_b, w_out, norm2_g, w_ff1, w_ff2):
    y_prompt, y_sample = x_prompt, x_sample
    prompt_rows, sample_rows = [], []
    for l in range(DEPTH):
        lam_init = 0.8 - 0.6 * math.exp(-0.3 * l)
        weights = (rel_bias, norm1_g[l], w_in[l], qn_a_g[l], kn_a_g[l], lam_q1[l], lam_k1[l], lam_q2[l],
                   lam_k2[l], subln_a_g[l], qn_b_g[l], kn_b_g[l], w_o_a[l], w_o_b[l], w_out[l], norm2_g[l],
                   w_ff1[l], w_ff2[l], lam_init)
        y_prompt, rp = layer_forward(y_prompt, None, *weights)
        past = (cache_a_k[l], cache_a_v[l], cache_b_k[l], cache_b_v[l], cache_b_kidx[l])
        y_sample, rs = layer_forward(y_sample, past, *weights)
        prompt_rows.append(rp)
        sample_rows.append(rs)
    p_a_k, p_a_v, p_b_k, p_b_v, p_b_kidx = (jnp.stack(r, axis=0) for r in zip(*prompt_rows))
    s_a_k, s_a_v, s_b_k, s_b_v, s_b_kidx = (jnp.stack(r, axis=0) for r in zip(*sample_rows))
    return (y_prompt, y_sample, p_a_k, p_a_v, p_b_k, p_b_v, p_b_kidx, s_a_k, s_a_v, s_b_k, s_b_v, s_b_kidx)
```

```python
import numpy as np
import concourse.bass as bass
import concourse.mybir as mybir
from concourse.bass_utils import run_bass_kernel_spmd

F32 = mybir.dt.float32
BF16 = mybir.dt.bfloat16
ALU = mybir.AluOpType
AF = mybir.ActivationFunctionType
AX = mybir.AxisListType

NCORES = 8
D = 2048
SEQ = 16384
NPT = 16
NST = 2
NT = NPT + NST
TOK = NT * 128
PAST = 1024
DEC = 32
H_A = 8
H_B = 8
HKV = 2
H_IDX = 16
D_IDX = 64
TOPK = 256
D_FF = 8192
EPS = 1e-6
IN_COLS = 9808
NEG = -30000.0
BIGNEG = -1.0e30

R_KTA = 0
R_VA = 1024
R_KTB = 2048
R_VB = 2304
R_KIT = 2560
NROWS = 2688
SK = 1152

C_QA, C_KA, C_VA, C_QB, C_KB, C_VB, C_QI, C_KI, C_WI, C_GA, C_GB = (
    0, 1024, 2048, 3072, 4096, 4352, 4608, 5632, 5696, 5712, 7760)


class Res:
    __slots__ = ("w", "r", "name")

    def __init__(self, name=""):
        self.w = None
        self.r = {}
        self.name = name


class KB:
    ENGS = ("pe", "act", "dve", "pool", "sp")

    def __init__(self, nc):
        self.nc = nc
        self.ops = {e: [] for e in self.ENGS}
        self.count = {e: 0 for e in self.ENGS}
        self.waited = {e: {} for e in self.ENGS}
        self.pending = {e: {} for e in self.ENGS}
        self.chans = {}
        self.res = []

    def R(self, name=""):
        r = Res(name)
        self.res.append(r)
        return r

    def chan(self, name):
        if name not in self.chans:
            self.chans[name] = 0
        return name

    def op(self, eng, fn, reads=(), writes=(), chan=None, cinc=16):
        need = dict(self.pending[eng])
        self.pending[eng] = {}

        def add(k, v):
            if need.get(k, 0) < v:
                need[k] = v

        for r in reads:
            if r.w is not None:
                add(*r.w)
        for w in writes:
            if w.w is not None:
                add(*w.w)
            for k, v in w.r.items():
                add(k, v)
        if chan is not None and self.chans[chan] > 0:
            add("c:" + chan, self.chans[chan])
        waits = []
        wd = self.waited[eng]
        for k, v in need.items():
            if eng == "pe" and k == "e:pe":
                continue
            if wd.get(k, 0) >= v:
                continue
            wd[k] = v
            waits.append((k, v))
        if chan is not None:
            self.chans[chan] += cinc
            tok = ("c:" + chan, self.chans[chan])
            inc = cinc
        else:
            self.count[eng] += 1
            tok = ("e:" + eng, self.count[eng])
            inc = 1
        self.ops[eng].append((fn, waits, tok[0], inc))
        for r in reads:
            if r.r.get(tok[0], 0) < tok[1]:
                r.r[tok[0]] = tok[1]
        for w in writes:
            w.w = tok
            w.r = {}
        return tok

    def barrier(self):
        toks = {}
        for e in self.ENGS:
            if self.count[e] > 0:
                toks["e:" + e] = self.count[e]
        for c, v in self.chans.items():
            if v > 0:
                toks["c:" + c] = v
        for e in self.ENGS:
            for k, v in toks.items():
                if self.pending[e].get(k, 0) < v:
                    self.pending[e][k] = v
        for r in self.res:
            r.w = None
            r.r = {}

    def emit(self):
        nc = self.nc
        import contextlib
        keys = ["e:" + e for e in self.ENGS] + ["c:" + c for c in self.chans]
        with contextlib.ExitStack() as es:
            sems = {}
            for i, k in enumerate(keys):
                sems[k] = es.enter_context(nc.semaphore("s%d" % i))
            block = es.enter_context(nc.Block())

            def run(eng, e):
                for fn, waits, tk, inc in self.ops[eng]:
                    for k, v in waits:
                        e.wait_ge(sems[k], v)
                    ins = fn(e)
                    ins.then_inc(sems[tk], inc)
                if eng == "sp":
                    for k in keys:
                        v = self.count[k[2:]] if k.startswith("e:") else self.chans[k[2:]]
                        if v > 0 and k != "e:sp":
                            e.wait_ge(sems[k], v)

            @block.tensor
            def _(e):
                run("pe", e)

            @block.scalar
            def _(e):
                run("act", e)

            @block.vector
            def _(e):
                run("dve", e)

            @block.gpsimd
            def _(e):
                run("pool", e)

            @block.sync
            def _(e):
                run("sp", e)


class Arena:
    def __init__(self, ap_bf16, nelem):
        self.ap = ap_bf16
        self.n = nelem
        self.off = 0

    def mark(self):
        return self.off

    def release(self, m):
        self.off = m

    def get(self, shape, dtype):
        n = 1
        for s in shape:
            n *= s
        nb = n * (4 if dtype == F32 else 2)
        nb = (nb + 63) // 64 * 64
        e0 = self.off // 2
        self.off += nb
        assert self.off // 2 <= self.n, "arena overflow %d" % self.off
        v = self.ap[:, e0:e0 + nb // 2]
        if dtype == F32:
            v = v.bitcast(F32)
        v = v[:, 0:n]
        if len(shape) == 2:
            v = v.rearrange("p (a b) -> p a b", b=shape[1])
        elif len(shape) == 3:
            v = v.rearrange("p (a b c) -> p a b c", b=shape[1], c=shape[2])
        return v


def build_program(stop_after=99):
    import os
    nc = bass.Bass("TRN2", target_bir_lowering=False)
    kb = KB(nc)

    def din(name, shape, dt=F32):
        return nc.dram_tensor(name, list(shape), dt, kind="ExternalInput").ap()

    def dout(name, shape, dt=F32):
        return nc.dram_tensor(name, list(shape), dt, kind="ExternalOutput").ap()

    xp = din("xp", [NPT * 128, D])
    xs = din("xs", [NST * DEC, D])
    cak = din("cak", [NST, PAST, 1024])
    cav = din("cav", [NST, PAST, 1024])
    cbk = din("cbk", [NST, PAST, 256])
    cbv = din("cbv", [NST, PAST, 256])
    cbi = din("cbi", [NST, PAST, 64])
    rel_bias = din("rel_bias", [32, 16])
    norm1_g = din("norm1_g", [D])
    w_in = din("w_in", [D + 1, IN_COLS])[0:D, :]
    qn_a_g = din("qn_a_g", [128])
    kn_a_g = din("kn_a_g", [128])
    lamv = din("lamv", [4, 64])
    subln_g = din("subln_a_g", [128])
    qn_b_g = din("qn_b_g", [128])
    kn_b_g = din("kn_b_g", [128])
    w_o_a = din("w_o_a", [1024 + 1, D])[0:1024, :]
    w_o_b = din("w_o_b", [1024 + 1, D])[0:1024, :]
    w_out = din("w_out", [D + 1, D])[0:D, :]
    norm2_g = din("norm2_g", [D])
    w_ff1 = din("w_ff1", [D + 1, D_FF])[0:D, :]
    w_ff2 = din("w_ff2", [D_FF + 1, D])[0:D_FF, :]
    c_ident = din("c_ident", [128, 128])
    c_sel = din("c_sel", [27])
    c_oh = din("c_oh", [32, 512])
    c_cm = din("c_cm", [128, 128])
    c_bn = din("c_bn", [16, 2, 128, 128])

    yp = dout("yp", [NPT * 128, D])
    ys = dout("ys", [NST * DEC, D])
    pak = dout("pak", [NPT * 128, 1024])
    pav = dout("pav", [NPT * 128, 1024])
    pbk = dout("pbk", [NPT * 128, 256])
    pbv = dout("pbv", [NPT * 128, 256])
    pbi = dout("pbi", [NPT * 128, 64])
    sak = dout("sak", [NST * DEC, 1024])
    sav = dout("sav", [NST * DEC, 1024])
    sbk = dout("sbk", [NST * DEC, 256])
    sbv = dout("sbv", [NST * DEC, 256])
    sbi = dout("sbi", [NST * DEC, 64])

    kvl = nc.dram_tensor("kvl", [NROWS, 2048], BF16)
    kva = nc.dram_tensor("kva", [NCORES * NROWS, 2048], BF16)
    kvs = nc.dram_tensor("kvs", [NST, NROWS, SK], BF16).ap()
    qTA = nc.dram_tensor("qTA", [8, 128, TOK], BF16).ap()
    qTB = nc.dram_tensor("qTB", [8, 128, TOK], BF16).ap()
    qTI = nc.dram_tensor("qTI", [8, 128, TOK], BF16).ap()
    gates = nc.dram_tensor("gates", [TOK, 4096], F32).ap()
    kvl_ap = kvl.ap()

    import contextlib
    es = contextlib.ExitStack()
    ARENA_ELEMS = 100 * 1024
    arena_t = es.enter_context(nc.sbuf_tensor("arena", [128, ARENA_ELEMS], BF16))
    ar = Arena(arena_t[:], ARENA_ELEMS)
    psf = [es.enter_context(nc.psum_tensor("ps%d" % i, [128, 512], F32)) for i in range(8)]
    psr = [kb.R("ps%d" % i) for i in range(8)]

    def ps_f32(i):
        return psf[i][:]

    def ps_bf16(i):
        return psf[i][:].bitcast(BF16)

    def tile_rows(t):
        return 128 if t < NPT else DEC

    def x_rows(t):
        if t < NPT:
            return xp[t * 128:(t + 1) * 128, :]
        s = t - NPT
        return xs[s * DEC:(s + 1) * DEC, :]

    ident = ar.get([128], BF16)
    r_ident = kb.R("ident")
    kb.op("pool", lambda e: e.dma_start(out=ident, in_=c_ident), writes=[r_ident], chan=kb.chan("c0"))
    g1bc = ar.get([D], F32)
    r_g1 = kb.R("g1")
    kb.op("sp", lambda e: e.dma_start(out=g1bc, in_=norm1_g.partition_broadcast(128)), writes=[r_g1],
          chan=kb.chan("c1"))
    gains = {}
    for nm, src in (("qa", qn_a_g), ("ka", kn_a_g), ("qb", qn_b_g), ("kb", kn_b_g), ("sub", subln_g)):
        tl = ar.get([128], F32)
        rr = kb.R("gain" + nm)
        kb.op("sp", (lambda tl, src: lambda e: e.dma_start(out=tl, in_=src.partition_broadcast(128)))(tl, src),
              writes=[rr], chan=kb.chan("c1"))
        gains[nm] = (tl, rr)
    wiT = ar.get([NT, 16], F32)
    r_wi = [kb.R("wi%d" % t) for t in range(NT)]

    m0 = ar.mark()
    hT = ar.get([16, TOK], BF16)
    r_hT = [kb.R("hT%d" % t) for t in range(NT)]
    m1 = ar.mark()
    xt = [ar.get([D], F32) for _ in range(2)]
    r_xt = [kb.R("xt%d" % i) for i in range(2)]
    hb = [ar.get([D], BF16) for _ in range(2)]
    r_hb = [kb.R("hb%d" % i) for i in range(2)]
    junk = ar.get([D], BF16)
    r_junk = kb.R("junk")
    ss = ar.get([8], F32)
    r_ss = kb.R("ss")

    def load_x(t):
        b = t % 2
        nv = tile_rows(t)
        kb.op("sp", lambda e: e.dma_start(out=xt[b][:nv, :], in_=x_rows(t)), writes=[r_xt[b]],
              chan=kb.chan("xl%d" % b))

    load_x(0)
    for t in range(NT):
        b = t % 2
        nv = tile_rows(t)
        if t + 1 < NT:
            load_x(t + 1)
        kb.op("act", lambda e, b=b, nv=nv: e.activation(out=junk[:nv, :], in_=xt[b][:nv, :], func=AF.Square,
                                                      accum_out=ss[:nv, 0:1]),
              reads=[r_xt[b]], writes=[r_junk, r_ss])
        kb.op("act", lambda e, nv=nv: e.activation(out=ss[:nv, 1:2], in_=ss[:nv, 0:1], func=AF.Ln,
                                                 bias=EPS, scale=1.0 / D), reads=[r_ss], writes=[r_ss])
        kb.op("act", lambda e, nv=nv: e.activation(out=ss[:nv, 2:3], in_=ss[:nv, 1:2], func=AF.Exp,
                                                 scale=-0.5), reads=[r_ss], writes=[r_ss])
        kb.op("dve", lambda e, b=b, nv=nv: e.scalar_tensor_tensor(out=hb[b][:nv, :], in0=xt[b][:nv, :],
                                                                scalar=ss[:nv, 2:3], in1=g1bc[:nv, :],
                                                                op0=ALU.mult, op1=ALU.mult),
              reads=[r_xt[b], r_ss, r_g1], writes=[r_hb[b]])
        for half in range(2):
            pi = 6 + half
            for kk in range(8):
                k = half * 8 + kk
                kb.op("pe", lambda e, b=b, nv=nv, k=k, kk=kk, pi=pi: e.transpose(
                    out=ps_bf16(pi)[:, kk * 128:kk * 128 + nv], in_=hb[b][:nv, k * 128:(k + 1) * 128],
                    identity=ident[:nv, :nv]),
                    reads=[r_hb[b], r_ident], writes=[psr[pi]])
            src = ps_bf16(pi).rearrange("p (a b) -> p a b", b=128)[:, :, :nv]
            dst = hT[:, half * 8:(half + 1) * 8, t * 128:t * 128 + nv]
            eng = "act" if half == 0 else "dve"
            if eng == "act":
                kb.op("act", lambda e, src=src, dst=dst: e.copy(out=dst, in_=src), reads=[psr[pi]], writes=[r_hT[t]])
            else:
                kb.op("dve", lambda e, src=src, dst=dst: e.tensor_copy(out=dst, in_=src), reads=[psr[pi]],
                      writes=[r_hT[t]])

    if stop_after <= 0:
        dbg = ar.get([D], F32)
        r_dbg = kb.R("dbg")
        kb.op("dve", lambda e: e.tensor_copy(out=dbg.rearrange("p (a b) -> p a b", b=128),
                                             in_=hT[:, :, 0:128]), reads=[r_hT[0]], writes=[r_dbg])
        kb.op("sp", lambda e: e.dma_start(out=yp[0:128, :], in_=dbg), reads=[r_dbg], chan=kb.chan("dbg"))
        kb.emit()
        es.close()
        return nc

    ar.release(m1)
    kb.barrier()

    class Ring:
        def __init__(self, name, n, shape, dtype):
            self.t = [ar.get(shape, dtype) for _ in range(n)]
            self.r = [kb.R("%s%d" % (name, i)) for i in range(n)]
            self.c = [kb.chan("%s%d" % (name, i)) for i in range(n)]
            self.i = -1
            self.n = n

        def next(self):
            self.i = (self.i + 1) % self.n
            return self.t[self.i], self.r[self.i], self.c[self.i]

    kb.op("dve", lambda e: e.tensor_scalar(out=gains["qa"][0], in0=gains["qa"][0], scalar1=0.125, scalar2=None,
                                           op0=ALU.mult), reads=[gains["qa"][1]], writes=[gains["qa"][1]])
    kb.op("dve", lambda e: e.tensor_scalar(out=gains["qb"][0], in0=gains["qb"][0], scalar1=float(128 ** -0.5),
                                           scalar2=None, op0=ALU.mult), reads=[gains["qb"][1]],
          writes=[gains["qb"][1]])

    wring = Ring("wb", 2, [16, 512], BF16)
    sqr = Ring("sq", 2, [512], F32)
    znr = Ring("zn", 3, [512], F32)
    zbr = Ring("zb", 3, [512], BF16)
    trr = Ring("tr", 3, [4, 128], BF16)
    st8 = ar.get([64], F32)
    r_st8 = kb.R("st8")
    r_dram = kb.R("dram_scratch")
    mm_banks = [0, 1, 2]
    tr_banks = [6, 7]
    cnt = {"mm": 0, "tr": 0}

    def transpose_store(src, r_src, nv, nch, dst_ap, dst_res=r_dram):
        pi = tr_banks[cnt["tr"] % 2]
        cnt["tr"] += 1
        for c in range(nch):
            kb.op("pe", lambda e, c=c, pi=pi: e.transpose(out=ps_bf16(pi)[:, c * 128:c * 128 + nv],
                                                          in_=src[:nv, c * 128:(c + 1) * 128],
                                                          identity=ident[:nv, :nv]),
                  reads=[r_src, r_ident], writes=[psr[pi]])
        tb, r_tb, c_tb = trr.next()
        kb.op("dve", lambda e, pi=pi, tb=tb: e.tensor_copy(
            out=tb[:, 0:nch, 0:nv], in_=ps_bf16(pi).rearrange("p (a b) -> p a b", b=128)[:, 0:nch, 0:nv]),
            reads=[psr[pi]], writes=[r_tb])
        kb.op("sp", lambda e, tb=tb: e.dma_start(out=dst_ap, in_=tb[:, 0:nch, 0:nv]), reads=[r_tb, dst_res],
              chan=c_tb)

    def group_norm(pi, nv, ncols, gs, gain, r_gain):
        ng = ncols // gs
        sq, r_sq, _ = sqr.next()
        kb.op("act", lambda e: e.activation(out=sq[:nv, :ncols], in_=ps_f32(pi)[:nv, :ncols], func=AF.Square),
              reads=[psr[pi]], writes=[r_sq])
        kb.op("dve", lambda e: e.tensor_reduce(out=st8[:nv, 0:ng],
                                               in_=sq[:nv, :ncols].rearrange("p (g d) -> p g d", d=gs),
                                               axis=AX.X, op=ALU.add), reads=[r_sq], writes=[r_st8])
        kb.op("act", lambda e: e.activation(out=st8[:nv, 16:16 + ng], in_=st8[:nv, 0:ng], func=AF.Ln, bias=EPS,
                                            scale=1.0 / gs), reads=[r_st8], writes=[r_st8])
        kb.op("act", lambda e: e.activation(out=st8[:nv, 32:32 + ng], in_=st8[:nv, 16:16 + ng], func=AF.Exp,
                                            scale=-0.5), reads=[r_st8], writes=[r_st8])
        zn, r_zn, c_zn = znr.next()
        kb.op("dve", lambda e: e.tensor_tensor(
            out=zn[:nv, :ncols].rearrange("p (g d) -> p g d", d=gs),
            in0=ps_f32(pi)[:nv, :ncols].rearrange("p (g d) -> p g d", d=gs),
            in1=st8[:nv, 32:32 + ng].unsqueeze(2).to_broadcast([nv, ng, gs]), op=ALU.mult),
            reads=[psr[pi], r_st8], writes=[r_zn])
        nrep = ncols // 128
        kb.op("pool", lambda e: e.tensor_tensor(
            out=zn[:nv, :ncols].rearrange("p (g d) -> p g d", d=128),
            in0=zn[:nv, :ncols].rearrange("p (g d) -> p g d", d=128),
            in1=gain[:nv, :].unsqueeze(1).to_broadcast([nv, nrep, 128]), op=ALU.mult),
            reads=[r_zn, r_gain], writes=[r_zn])
        return zn, r_zn, c_zn

    def to_bf16(src, r_src, nv, ncols):
        zb, r_zb, c_zb = zbr.next()
        kb.op("act", lambda e: e.copy(out=zb[:nv, :ncols], in_=src[:nv, :ncols]), reads=[r_src], writes=[r_zb])
        return zb, r_zb, c_zb

    def out_rows(t, pten, sten, c0, ncols):
        if t < NPT:
            return pten[t * 128:(t + 1) * 128, c0:c0 + ncols]
        s = t - NPT
        return sten[s * DEC:(s + 1) * DEC, c0:c0 + ncols]

    def kt_dst(t, row0, nch):
        if t < NPT:
            return kvl_ap[row0:row0 + nch * 128, t * 128:(t + 1) * 128].rearrange("(h d) t -> d h t", d=128)
        s = t - NPT
        return kvs[s, row0:row0 + nch * 128, PAST:PAST + DEC].rearrange("(h d) t -> d h t", d=128)

    def v_dst(t, row0, nh):
        if t < NPT:
            return kvl_ap[row0:row0 + nh * 128, t * 128:(t + 1) * 128].rearrange("(h p) d -> p h d", p=128)
        s = t - NPT
        return kvs[s, row0:row0 + nh * 128, PAST:PAST + 128].rearrange("(h p) d -> p h d", p=128)[0:DEC]

    def evac(kind, b, t, pi, ncols):
        nv = tile_rows(t)
        cols = slice(t * 128, t * 128 + nv)
        if kind in ("qA", "qB"):
            g = gains["qa"] if kind == "qA" else gains["qb"]
            zn, r_zn, _ = group_norm(pi, nv, 512, 64 if kind == "qA" else 128, g[0], g[1])
            zb, r_zb, _ = to_bf16(zn, r_zn, nv, 512)
            qd = qTA if kind == "qA" else qTB
            transpose_store(zb, r_zb, nv, 4, qd[4 * b:4 * b + 4, :, cols].rearrange("h d t -> d h t"))
        elif kind == "kA":
            zn, r_zn, c_zn = group_norm(pi, nv, 512, 64, gains["ka"][0], gains["ka"][1])
            kb.op("sp", lambda e: e.dma_start(out=out_rows(t, pak, sak, b * 512, 512), in_=zn[:nv, :512]),
                  reads=[r_zn], chan=c_zn)
            zb, r_zb, _ = to_bf16(zn, r_zn, nv, 512)
            transpose_store(zb, r_zb, nv, 4, kt_dst(t, R_KTA + 4 * b * 128, 4))
        elif kind == "vA":
            zn, r_zn, c_zn = znr.next()
            kb.op("dve", lambda e: e.tensor_copy(out=zn[:nv, :512], in_=ps_f32(pi)[:nv, :512]), reads=[psr[pi]],
                  writes=[r_zn])
            kb.op("sp", lambda e: e.dma_start(out=out_rows(t, pav, sav, b * 512, 512), in_=zn[:nv, :512]),
                  reads=[r_zn], chan=c_zn)
            zb, r_zb, c_zb = to_bf16(zn, r_zn, nv, 512)
            kb.op("sp", lambda e: e.dma_start(out=v_dst(t, R_VA + 4 * b * 128, 4),
                                              in_=zb[:nv, :512].rearrange("p (h d) -> p h d", d=128)),
                  reads=[r_zb, r_dram], chan=c_zb)
        elif kind == "kvB":
            zn, r_zn, c_zn = group_norm(pi, nv, 256, 128, gains["kb"][0], gains["kb"][1])
            kb.op("dve", lambda e: e.tensor_copy(out=zn[:nv, 256:512], in_=ps_f32(pi)[:nv, 256:512]),
                  reads=[psr[pi]], writes=[r_zn])
            kb.op("sp", lambda e: e.dma_start(out=out_rows(t, pbk, sbk, 0, 256), in_=zn[:nv, 0:256]),
                  reads=[r_zn], chan=c_zn)
            kb.op("sp", lambda e: e.dma_start(out=out_rows(t, pbv, sbv, 0, 256), in_=zn[:nv, 256:512]),
                  reads=[r_zn], chan=c_zn)
            zb, r_zb, c_zb = to_bf16(zn, r_zn, nv, 512)
            kb.op("sp", lambda e: e.dma_start(out=v_dst(t, R_VB, 2),
                                              in_=zb[:nv, 256:512].rearrange("p (h d) -> p h d", d=128)),
                  reads=[r_zb, r_dram], chan=c_zb)
            transpose_store(zb, r_zb, nv, 2, kt_dst(t, R_KTB, 2))
        elif kind == "qi":
            zn, r_zn, c_zn = znr.next()
            kb.op("dve", lambda e: e.tensor_copy(out=zn[:nv, :512], in_=ps_f32(pi)[:nv, :512]), reads=[psr[pi]],
                  writes=[r_zn])
            zb, r_zb, _ = to_bf16(zn, r_zn, nv, 512)
            transpose_store(zb, r_zb, nv, 4, qTI[4 * b:4 * b + 4, :, cols].rearrange("h d t -> d h t"))
        elif kind == "kiwi":
            zn, r_zn, c_zn = znr.next()
            kb.op("dve", lambda e: e.tensor_copy(out=zn[:nv, 0:80], in_=ps_f32(pi)[:nv, 0:80]), reads=[psr[pi]],
                  writes=[r_zn])
            kb.op("sp", lambda e: e.dma_start(out=out_rows(t, pbi, sbi, 0, 64), in_=zn[:nv, 0:64]),
                  reads=[r_zn], chan=c_zn)
            kb.op("pool", lambda e: e.tensor_copy(out=wiT[:nv, t, :], in_=zn[:nv, 64:80]), reads=[r_zn],
                  writes=[r_wi[t]])
            zb, r_zb, _ = zbr.next()
            kb.op("act", lambda e: e.copy(out=zb[:nv, 0:64], in_=zn[:nv, 0:64]), reads=[r_zn], writes=[r_zb])
            kb.op("act", lambda e: e.copy(out=zb[:nv, 64:128], in_=zn[:nv, 0:64]), reads=[r_zn], writes=[r_zb])
            transpose_store(zb, r_zb, nv, 1, kt_dst(t, R_KIT, 1))
        elif kind in ("ga", "gb"):
            zn, r_zn, c_zn = znr.next()
            kb.op("act", lambda e: e.activation(out=zn[:nv, :512], in_=ps_f32(pi)[:nv, :512], func=AF.Exp,
                                                scale=-1.0), reads=[psr[pi]], writes=[r_zn])
            kb.op("dve", lambda e: e.tensor_scalar(out=zn[:nv, :512], in0=zn[:nv, :512], scalar1=1.0, scalar2=None,
                                                   op0=ALU.add), reads=[r_zn], writes=[r_zn])
            kb.op("dve", lambda e: e.reciprocal(out=zn[:nv, :512], in_=zn[:nv, :512]), reads=[r_zn], writes=[r_zn])
            off = (0 if kind == "ga" else 2048) + b * 512
            kb.op("sp", lambda e: e.dma_start(out=gates[t * 128:t * 128 + nv, off:off + 512], in_=zn[:nv, :512]),
                  reads=[r_zn, r_dram], chan=c_zn)

    def cache_conv():
        cbr = Ring("cb", 2, [1024], BF16)
        import os
        for s in range(NST if (stop_after != 1 or os.environ.get('KB_CC')) else 0):
            for jb in range(PAST // 128):
                rows = slice(jb * 128, (jb + 1) * 128)
                kc = slice(jb * 128, (jb + 1) * 128)
                cb, r_cb, c_cb = cbr.next()
                kb.op("pool", lambda e, cb=cb, s=s, rows=rows: e.dma_start(out=cb[:, :], in_=cak[s, rows, :]),
                      writes=[r_cb], chan=c_cb)
                for hh in range(2):
                    transpose_store(cb[:, hh * 512:(hh + 1) * 512], r_cb, 128, 4,
                                    kvs[s, R_KTA + hh * 512:R_KTA + (hh + 1) * 512, kc].rearrange(
                                        "(h d) t -> d h t", d=128))
                cb, r_cb, c_cb = cbr.next()
                kb.op("pool", lambda e, cb=cb, s=s, rows=rows: e.dma_start(out=cb[:, :], in_=cav[s, rows, :]),
                      writes=[r_cb], chan=c_cb)
                kb.op("sp", lambda e, cb=cb, s=s, kc=kc: e.dma_start(
                    out=kvs[s, R_VA:R_VA + 1024, kc].rearrange("(h p) d -> p h d", p=128),
                    in_=cb[:, :].rearrange("p (h d) -> p h d", d=128)), reads=[r_cb, r_dram], chan=c_cb)
                cb, r_cb, c_cb = cbr.next()
                kb.op("pool", lambda e, cb=cb, s=s, rows=rows: e.dma_start(out=cb[:, 0:256], in_=cbk[s, rows, :]),
                      writes=[r_cb], chan=c_cb)
                kb.op("pool", lambda e, cb=cb, s=s, rows=rows: e.dma_start(out=cb[:, 256:512], in_=cbv[s, rows, :]),
                      writes=[r_cb], chan=c_cb)
                kb.op("pool", lambda e, cb=cb, s=s, rows=rows: e.dma_start(out=cb[:, 512:576], in_=cbi[s, rows, :]),
                      writes=[r_cb], chan=c_cb)
                kb.op("pool", lambda e, cb=cb, s=s, rows=rows: e.dma_start(out=cb[:, 576:640], in_=cbi[s, rows, :]),
                      writes=[r_cb], chan=c_cb)
                transpose_store(cb[:, 0:256], r_cb, 128, 2,
                                kvs[s, R_KTB:R_KTB + 256, kc].rearrange("(h d) t -> d h t", d=128))
                kb.op("sp", lambda e, cb=cb, s=s, kc=kc: e.dma_start(
                    out=kvs[s, R_VB:R_VB + 256, kc].rearrange("(h p) d -> p h d", p=128),
                    in_=cb[:, 256:512].rearrange("p (h d) -> p h d", d=128)), reads=[r_cb, r_dram], chan=c_cb)
                transpose_store(cb[:, 512:640], r_cb, 128, 1,
                                kvs[s, R_KIT:R_KIT + 128, kc].rearrange("(h d) t -> d h t", d=128))

    blocks = [("kA", 0, C_KA, 512), ("kA", 1, C_KA + 512, 512), ("vA", 0, C_VA, 512), ("vA", 1, C_VA + 512, 512),
              ("kvB", 0, C_KB, 512), ("kiwi", 0, C_KI, 80), ("GATHER", 0, 0, 0),
              ("qA", 0, C_QA, 512), ("qA", 1, C_QA + 512, 512), ("qB", 0, C_QB, 512), ("qB", 1, C_QB + 512, 512),
              ("qi", 0, C_QI, 512), ("qi", 1, C_QI + 512, 512)]
    for b in range(4):
        blocks.append(("ga", b, C_GA + b * 512, 512))
    for b in range(4):
        blocks.append(("gb", b, C_GB + b * 512, 512))
    if stop_after == 1:
        import os
        blocks = [blocks[int(i)] for i in os.environ.get('KB_BL', '0,1,2,3,4,5,6').split(',')]
    r_kva = kb.R("kva")
    post_gather = [False]
    stgr = Ring("stg", 2, [16, 256], F32)
    for kind, b, c0, ncols in blocks:
        if kind == "GATHER":
            cache_conv()
            post_gather[0] = True
            if os.environ.get("K_DBG1"):
                continue
            kb.op("pool", lambda e: e.collective_compute(
                "AllGather", ALU.bypass, replica_groups=[list(range(NCORES))],
                ins=[kvl.ap().opt()], outs=[kva.ap().opt()]), writes=[r_dram, r_kva], chan=kb.chan("cc"), cinc=1)
            continue
        wb, r_wb, c_wb = wring.next()
        if not post_gather[0]:
            kb.op("pool", lambda e, wb=wb, c0=c0, ncols=ncols: e.dma_start(
                out=wb[:, :, :ncols], in_=w_in[:, c0:c0 + ncols].rearrange("(k p) n -> p k n", p=128)),
                writes=[r_wb], chan=c_wb)
        else:
            for hf in range(2):
                stg, r_stg, c_stg = stgr.next()
                cs = slice(hf * 256, (hf + 1) * 256)
                kb.op("sp", lambda e, stg=stg, c0=c0, hf=hf: e.dma_start(
                    out=stg, in_=w_in[:, c0 + hf * 256:c0 + (hf + 1) * 256].rearrange("(k p) n -> p k n", p=128)),
                    writes=[r_stg], chan=c_stg)
                kb.op("pool", lambda e, stg=stg, wb=wb, cs=cs: e.tensor_copy(out=wb[:, :, cs], in_=stg),
                      reads=[r_stg], writes=[r_wb])
        for t in range(NT):
            nv = tile_rows(t)
            pi = mm_banks[cnt["mm"] % 3]
            cnt["mm"] += 1
            for k in range(16):
                kb.op("pe", lambda e, k=k, t=t, nv=nv, pi=pi, wb=wb, ncols=ncols: e.matmul(
                    ps_f32(pi)[:nv, :ncols], lhsT=hT[:, k, t * 128:t * 128 + nv], rhs=wb[:, k, :ncols],
                    start=(k == 0), stop=(k == 15)), reads=[r_hT[t], r_wb], writes=[psr[pi]])
            evac(kind, b, t, pi, ncols)
    kb.barrier()
    ar.release(m0)
    m2 = ar.mark()
    oTA = nc.dram_tensor("oTA", [8, 128, TOK], BF16).ap()
    oTB = nc.dram_tensor("oTB", [8, 128, TOK], BF16).ap()
    nmask = nc.dram_tensor("nmask", [NT, 128, 16384], BF16).ap()
    x1s = nc.dram_tensor("x1s", [TOK, D], F32).ap()
    kva_ap = kva.ap()
    kva_x = kva_ap.rearrange("(r x) c -> x r c", x=NROWS)

    selc = ar.get([27], F32)
    r_sel = kb.R("sel")
    kb.op("sp", lambda e: e.dma_start(out=selc, in_=c_sel.partition_broadcast(128)), writes=[r_sel],
          chan=kb.chan("c1"))
    b15 = ar.get([16], F32)
    r_b15 = kb.R("b15")
    kb.op("sp", lambda e: e.dma_start(out=b15, in_=rel_bias[15, :].partition_broadcast(128)), writes=[r_b15],
          chan=kb.chan("c1"))
    cm = ar.get([128], F32)
    r_cm = kb.R("cm")
    kb.op("sp", lambda e: e.dma_start(out=cm, in_=c_cm), writes=[r_cm], chan=kb.chan("c1"))
    lamt = ar.get([4, 64], F32)
    r_lam = kb.R("lam")
    kb.op("sp", lambda e: e.dma_start(out=lamt, in_=lamv.rearrange("a b -> (a b)").partition_broadcast(128)),
          writes=[r_lam], chan=kb.chan("c1"))
    lsc = ar.get([16], F32)
    junk64 = ar.get([64], F32)
    r_j64 = kb.R("j64")
    for i in range(2):
        kb.op("dve", lambda e, i=i: e.tensor_tensor(out=junk64, in0=lamt[:, 2 * i, :], in1=lamt[:, 2 * i + 1, :],
                                                    op=ALU.mult), reads=[r_lam], writes=[r_j64])
        kb.op("dve", lambda e, i=i: e.tensor_reduce(out=lsc[:, i:i + 1], in_=junk64, axis=AX.X, op=ALU.add),
              reads=[r_j64], writes=[r_lam])
        kb.op("act", lambda e, i=i: e.activation(out=lsc[:, 2 + i:3 + i], in_=lsc[:, i:i + 1], func=AF.Exp),
              reads=[r_lam], writes=[r_lam])
    LAM_INIT = 0.8 - 0.6 * 1.0
    kb.op("dve", lambda e: e.tensor_tensor(out=lsc[:, 4:5], in0=lsc[:, 2:3], in1=lsc[:, 3:4], op=ALU.subtract),
          reads=[r_lam], writes=[r_lam])
    kb.op("dve", lambda e: e.tensor_scalar(out=lsc[:, 5:6], in0=lsc[:, 4:5], scalar1=LAM_INIT, scalar2=-1.0,
                                           op0=ALU.add, op1=ALU.mult), reads=[r_lam], writes=[r_lam])
    idxadd = ar.get([8, 128], F32)
    r_idxadd = kb.R("idxadd")
    for r in range(8):
        kb.op("dve", lambda e, r=r: e.tensor_scalar(out=idxadd[:, r, :], in0=cm, scalar1=selc[:, 9 + r:10 + r],
                                                    scalar2=BIGNEG / NEG, op0=ALU.mult, op1=ALU.mult),
              reads=[r_cm, r_sel], writes=[r_idxadd])
        kb.op("dve", lambda e, r=r: e.tensor_scalar(out=junk64[:, 0:1], in0=selc[:, 18 + r:19 + r], scalar1=BIGNEG,
                                                    scalar2=None, op0=ALU.mult), reads=[r_sel], writes=[r_j64])
        kb.op("dve", lambda e, r=r: e.tensor_scalar(out=idxadd[:, r, :], in0=idxadd[:, r, :],
                                                    scalar1=junk64[:, 0:1], scalar2=None, op0=ALU.add),
              reads=[r_j64, r_idxadd], writes=[r_idxadd])
    bnr = ar.get([16, 2, 128], F32)
    r_bnr = kb.R("bnr")
    kb.op("sp", lambda e: e.dma_start(out=bnr, in_=c_bn.rearrange("h t q s -> q h t s")), writes=[r_bnr],
          chan=kb.chan("c1"))
    for h in range(16):
        kb.op("dve", lambda e, h=h: e.tensor_scalar(out=bnr[:, h, :, :], in0=bnr[:, h, :, :],
                                                    scalar1=b15[:, h:h + 1], scalar2=None, op0=ALU.subtract),
              reads=[r_bnr, r_b15], writes=[r_bnr])
        kb.op("dve", lambda e, h=h: e.tensor_tensor(out=bnr[:, h, 0, :], in0=bnr[:, h, 0, :], in1=cm, op=ALU.add),
              reads=[r_bnr, r_cm], writes=[r_bnr])
    m2b = ar.mark()

    def tile_ctx(t):
        out = []
        if t < NPT:
            j = t
            for r in range(8):
                for jb in range(j + 1):
                    kind = "far"
                    if jb == j:
                        kind = r
                    elif jb == j - 1 and r == 7:
                        kind = 8
                    out.append((kind, r, jb, 128))
        else:
            for jb in range(9):
                kind = "far"
                if jb == 7:
                    kind = "sprev"
                if jb == 8:
                    kind = "snew"
                out.append((kind, 0, jb, 128 if jb < 8 else DEC))
        return out

    kit = ar.get([8, 2048], BF16)
    r_kit = kb.R("kit")
    kb.op("sp", lambda e: e.dma_start(out=kit, in_=kva_x[R_KIT:R_KIT + 128, :, :]), reads=[r_kva], writes=[r_kit],
          chan=kb.chan("kit"))
    kits = ar.get([SK], BF16)
    r_kits = kb.R("kits")
    sc = ar.get([16384], F32)
    r_sc = kb.R("sc")
    jbig = ar.get([16384], BF16)
    r_jbig = kb.R("jbig")
    qit = ar.get([8, 128], BF16)
    r_qit = kb.R("qit")
    tmpr = Ring("itmp", 3, [512], F32)
    bs = ar.get([16], F32)
    r_bs = kb.R("bs")
    NITER = 22
    idx_banks = [0, 1, 2, 3]
    for t in range(NT):
        nv = tile_rows(t)
        cols = slice(t * 128, t * 128 + nv)
        kb.op("sp", lambda e, cols=cols, nv=nv: e.dma_start(out=qit[:, :, :nv],
                                                            in_=qTI[:, :, cols].rearrange("h d t -> d h t")),
              writes=[r_qit], chan=kb.chan("qit"))
        if t >= NPT:
            s = t - NPT
            kb.op("sp", lambda e, s=s: e.dma_start(out=kits, in_=kvs[s, R_KIT:R_KIT + 128, :]), writes=[r_kits],
                  chan=kb.chan("kit"))
            segs = [(kits, r_kits, 0, 512, 0), (kits, r_kits, 512, 512, 512), (kits, r_kits, 1024, DEC, 1024)]
            ntot = PAST + DEC
        else:
            j = t
            L = (j + 1) * 128
            segs = []
            for r in range(8):
                for c0 in range(0, L, 512):
                    n = min(512, L - c0)
                    segs.append((kit[:, r, :], r_kit, c0, n, r * L + c0))
            ntot = 8 * L
        for (ksrc, r_ks, c0, n, o0) in segs:
            for hd in range(16):
                pr, hf = hd // 2, hd % 2
                pi = idx_banks[cnt["mm"] % 4]
                cnt["mm"] += 1
                kb.op("pe", lambda e, pr=pr, hf=hf, pi=pi, ksrc=ksrc, c0=c0, n=n, nv=nv: e.matmul(
                    ps_f32(pi)[:nv, :n], lhsT=qit[hf * 64:(hf + 1) * 64, pr, :nv],
                    rhs=ksrc[hf * 64:(hf + 1) * 64, c0:c0 + n], start=True, stop=True),
                    reads=[r_qit, r_ks], writes=[psr[pi]])
                if hd == 0:
                    kb.op("dve", lambda e, pi=pi, n=n, o0=o0, nv=nv, t=t: e.tensor_scalar(
                        out=sc[:nv, o0:o0 + n], in0=ps_f32(pi)[:nv, :n], scalar1=0.0, scalar2=wiT[:nv, t, 0:1],
                        op0=ALU.max, op1=ALU.mult), reads=[psr[pi], r_wi[t]], writes=[r_sc])
                else:
                    tm, r_tm, _ = tmpr.next()
                    kb.op("dve", lambda e, pi=pi, n=n, nv=nv, t=t, hd=hd, tm=tm: e.tensor_scalar(
                        out=tm[:nv, :n], in0=ps_f32(pi)[:nv, :n], scalar1=0.0, scalar2=wiT[:nv, t, hd:hd + 1],
                        op0=ALU.max, op1=ALU.mult), reads=[psr[pi], r_wi[t]], writes=[r_tm])
                    kb.op("pool", lambda e, n=n, o0=o0, nv=nv, tm=tm: e.tensor_tensor(
                        out=sc[:nv, o0:o0 + n], in0=sc[:nv, o0:o0 + n], in1=tm[:nv, :n], op=ALU.add),
                        reads=[r_tm, r_sc], writes=[r_sc])
        kb.op("dve", lambda e, nv=nv, ntot=ntot: e.tensor_reduce(out=bs[:nv, 5:6], in_=sc[:nv, :ntot], axis=AX.X,
                                                                 op=ALU.max), reads=[r_sc], writes=[r_bs])
        kb.op("dve", lambda e, nv=nv, ntot=ntot: e.tensor_reduce(out=bs[:nv, 0:1], in_=sc[:nv, :ntot], axis=AX.X,
                                                                 op=ALU.min), reads=[r_sc], writes=[r_bs])
        kb.op("dve", lambda e, nv=nv: e.tensor_tensor(out=bs[:nv, 1:2], in0=bs[:nv, 5:6], in1=bs[:nv, 0:1],
                                                      op=ALU.subtract), reads=[r_bs], writes=[r_bs])
        kb.op("dve", lambda e, nv=nv: e.tensor_scalar(out=bs[:nv, 1:2], in0=bs[:nv, 1:2], scalar1=0.5,
                                                      scalar2=1e-6, op0=ALU.mult, op1=ALU.add),
              reads=[r_bs], writes=[r_bs])
        if t < NPT:
            L = (t + 1) * 128
            scv = sc[:, 0:8 * L].rearrange("p (r l) -> p r l", l=L)[:, :, t * 128:(t + 1) * 128]
            kb.op("pool", lambda e, scv=scv: e.tensor_tensor(out=scv, in0=scv, in1=idxadd, op=ALU.add),
                  reads=[r_sc, r_idxadd], writes=[r_sc])
        for it in range(NITER):
            kb.op("dve", lambda e, nv=nv: e.tensor_tensor(out=bs[:nv, 2:3], in0=bs[:nv, 0:1], in1=bs[:nv, 1:2],
                                                          op=ALU.add), reads=[r_bs], writes=[r_bs])
            kb.op("dve", lambda e, nv=nv, ntot=ntot: e.tensor_scalar(
                out=jbig[:nv, :ntot], in0=sc[:nv, :ntot], scalar1=bs[:nv, 2:3], scalar2=None, op0=ALU.is_ge,
                op1=ALU.add, accum_out=bs[:nv, 3:4]), reads=[r_sc, r_bs], writes=[r_jbig, r_bs])
            kb.op("dve", lambda e, nv=nv: e.tensor_scalar(out=bs[:nv, 4:5], in0=bs[:nv, 3:4], scalar1=float(TOPK),
                                                          scalar2=bs[:nv, 1:2], op0=ALU.is_ge, op1=ALU.mult),
                  reads=[r_bs], writes=[r_bs])
            kb.op("dve", lambda e, nv=nv: e.tensor_tensor(out=bs[:nv, 0:1], in0=bs[:nv, 0:1], in1=bs[:nv, 4:5],
                                                          op=ALU.add), reads=[r_bs], writes=[r_bs])
            kb.op("dve", lambda e, nv=nv: e.tensor_scalar(out=bs[:nv, 1:2], in0=bs[:nv, 1:2], scalar1=0.5,
                                                          scalar2=None, op0=ALU.mult), reads=[r_bs], writes=[r_bs])
        kb.op("dve", lambda e, nv=nv, ntot=ntot: e.tensor_scalar(
            out=jbig[:nv, :ntot], in0=sc[:nv, :ntot], scalar1=bs[:nv, 0:1], scalar2=NEG, op0=ALU.is_lt,
            op1=ALU.mult), reads=[r_sc, r_bs], writes=[r_jbig])
        kb.op("sp", lambda e, nv=nv, ntot=ntot, t=t: e.dma_start(out=nmask[t, :nv, :ntot], in_=jbig[:nv, :ntot]),
              reads=[r_jbig, r_dram], chan=kb.chan("nm"))
    kb.barrier()
    ar.release(m2b)
    import os
    if os.environ.get("KDBG") == "2a":
        dbgb = ar.get([2048], BF16)
        dbg = ar.get([2048], F32)
        r_dbg = kb.R("dbg")
        kb.op("sp", lambda e: e.dma_start(out=dbgb, in_=nmask[2, :, 0:2048]), writes=[r_dbg], chan=kb.chan("dbg"))
        kb.op("dve", lambda e: e.tensor_copy(out=dbg, in_=dbgb), reads=[r_dbg], writes=[r_dbg])
        kb.op("sp", lambda e: e.dma_start(out=yp[0:128, :], in_=dbg), reads=[r_dbg], chan=kb.chan("dbg"))
        kb.op("sp", lambda e: e.dma_start(out=dbgb, in_=nmask[2, :, 2048:4096]), reads=[r_dbg], writes=[r_dbg], chan=kb.chan("dbg"))
        kb.op("dve", lambda e: e.tensor_copy(out=dbg, in_=dbgb), reads=[r_dbg], writes=[r_dbg])
        kb.op("sp", lambda e: e.dma_start(out=yp[128:256, :], in_=dbg), reads=[r_dbg], chan=kb.chan("dbg"))
        kb.emit()
        es.close()
        return nc
    kth = ar.get([8, 2048], BF16)
    r_kth = kb.R("kth")
    vth = ar.get([128, 130], BF16)
    r_vth = kb.R("vth")
    kts = [ar.get([SK], BF16) for _ in range(NST)]
    vts = [ar.get([9, 130], BF16) for _ in range(NST)]
    qth = ar.get([TOK], BF16)
    r_qth = kb.R("qth")
    qz = ar.get([2, TOK], BF16)
    nmt = ar.get([16384], BF16)
    r_nmt = kb.R("nmt")
    addt = ar.get([9, 128], BF16)
    r_addt = kb.R("addt")
    addf = ar.get([128], F32)
    r_addf = kb.R("addf")
    bnb = ar.get([2, 128], BF16)
    r_bnb = kb.R("bnb")
    negs = ar.get([9], F32)
    r_negs = kb.R("negs")
    ptr = Ring("pT", 3, [256], BF16)
    pfr = Ring("pF", 3, [256], F32)
    osr = Ring("os", 2, [128], F32)
    o2r = Ring("o2", 2, [128], F32)
    fs = ar.get([16], F32)
    r_fs = kb.R("fs")
    kb.op("dve", lambda e: e.memset(vth[:, :, 128:129], 1.0), writes=[r_vth])
    kb.op("dve", lambda e: e.memset(qz, 0.0), writes=[r_qth])
    for s in range(NST):
        kb.op("dve", lambda e, s=s: e.memset(vts[s][:, :, 128:129], 1.0), writes=[r_vth])
    kb.op("dve", lambda e: e.tensor_scalar(out=negs, in0=selc[:, 18:27], scalar1=NEG, scalar2=None, op0=ALU.mult),
          reads=[r_sel], writes=[r_negs])
    kb.op("dve", lambda e: e.tensor_scalar(out=gains["sub"][0], in0=gains["sub"][0], scalar1=1.0 - LAM_INIT,
                                           scalar2=None, op0=ALU.mult), reads=[gains["sub"][1]],
          writes=[gains["sub"][1]])
    s_banks = [0, 1, 2, 3]

    def load_head(mixer, hh):
        rk = (R_KTA if mixer == "A" else R_KTB) + hh * 128
        rv = (R_VA if mixer == "A" else R_VB) + hh * 128
        kb.op("sp", lambda e: e.dma_start(out=kth, in_=kva_x[rk:rk + 128, :, :]), reads=[r_kva], writes=[r_kth],
              chan=kb.chan("kth"))
        for r in range(8):
            kb.op("sp", lambda e, r=r: e.dma_start(out=vth[:, r * 16:(r + 1) * 16, 0:128],
                                                   in_=kva_x[rv:rv + 128, r, :].rearrange("p (j d) -> p j d", d=128)),
                  reads=[r_kva], writes=[r_vth], chan=kb.chan("vth"))
        for s in range(NST):
            kb.op("sp", lambda e, s=s: e.dma_start(out=kts[s], in_=kvs[s, rk:rk + 128, :]), writes=[r_kth],
                  chan=kb.chan("kth"))
            kb.op("sp", lambda e, s=s: e.dma_start(out=vts[s][:, :, 0:128],
                                                   in_=kvs[s, rv:rv + 128, :].rearrange("p (j d) -> p j d", d=128)),
                  writes=[r_vth], chan=kb.chan("vth"))

    def prep_head(h16, qsrc, hq):
        if h16 < 8:
            kb.op("sp", lambda e: e.dma_start(out=qz[0:64, 0, :], in_=qsrc[hq, 0:64, :]), writes=[r_qth],
                  chan=kb.chan("qth"))
            kb.op("sp", lambda e: e.dma_start(out=qz[64:128, 1, :], in_=qsrc[hq, 64:128, :]), writes=[r_qth],
                  chan=kb.chan("qth"))
        else:
            kb.op("sp", lambda e: e.dma_start(out=qth, in_=qsrc[hq]), writes=[r_qth], chan=kb.chan("qth"))
        for slot in range(9):
            kb.op("pool", lambda e, slot=slot: e.tensor_scalar(
                out=addf, in0=bnr[:, h16, 1, :], scalar1=selc[:, slot:slot + 1], scalar2=negs[:, slot:slot + 1],
                op0=ALU.mult, op1=ALU.add), reads=[r_bnr, r_sel, r_negs], writes=[r_addf])
            kb.op("dve", lambda e, slot=slot: e.scalar_tensor_tensor(
                out=addt[:, slot, :], in0=bnr[:, h16, 0, :], scalar=selc[:, 9 + slot:10 + slot], in1=addf,
                op0=ALU.mult, op1=ALU.add), reads=[r_bnr, r_sel, r_addf], writes=[r_addt])
        kb.op("pool", lambda e: e.tensor_copy(out=bnb, in_=bnr[:, h16, :, :]), reads=[r_bnr], writes=[r_bnb])

    def attend(mixer, h16, hq, t):
        nv = tile_rows(t)
        cols = slice(t * 128, t * 128 + nv)
        ctx = tile_ctx(t)
        ncomp = 2 if mixer == "A" else 1
        kd = 64 if mixer == "A" else 128
        L = (t + 1) * 128
        if mixer == "B":
            ntot = 8 * L if t < NPT else PAST + DEC
            kb.op("sp", lambda e: e.dma_start(out=nmt[:nv, :ntot], in_=nmask[t, :nv, :ntot]), writes=[r_nmt],
                  chan=kb.chan("nmt"))
        obank = [4, 5]
        for i, (kind, r, jb, n) in enumerate(ctx):
            if t < NPT:
                kblk = kth[:, r, jb * 128:jb * 128 + n]
                vblk = vth[:n, r * 16 + jb, 0:129]
                o0 = r * L + jb * 128
            else:
                s = t - NPT
                kblk = kts[s][:, jb * 128:jb * 128 + n]
                vblk = vts[s][:n, jb, 0:129]
                o0 = jb * 128
            pi = s_banks[i % 4]
            special = kind != "far"
            if kind == "sprev":
                addap = bnb[:nv, 1, :n]
                r_add = r_bnb
            elif kind == "snew":
                addap = bnb[:nv, 0, :n]
                r_add = r_bnb
            elif special:
                addap = addt[:nv, kind, :n]
                r_add = r_addt
            for c in range(ncomp):
                last = not special and mixer == "A"
                kb.op("pe", lambda e, c=c, pi=pi, kblk=kblk, n=n, last=last: e.matmul(
                    ps_f32(pi)[:n, c * nv:(c + 1) * nv], lhsT=kblk[:, :n],
                    rhs=(qz[:, c, cols] if mixer == "A" else qth[:, cols]), start=True, stop=last),
                    reads=[r_kth, r_qth], writes=[psr[pi]])
                if mixer == "B":
                    kb.op("pe", lambda e, c=c, pi=pi, n=n, o0=o0, special=special: e.matmul(
                        ps_f32(pi)[:n, c * nv:(c + 1) * nv], lhsT=nmt[:nv, o0:o0 + n], rhs=ident[:nv, :nv],
                        start=False, stop=not special), reads=[r_nmt, r_ident], writes=[psr[pi]])
                if special:
                    kb.op("pe", lambda e, c=c, pi=pi, n=n, addap=addap: e.matmul(
                        ps_f32(pi)[:n, c * nv:(c + 1) * nv], lhsT=addap, rhs=ident[:nv, :nv],
                        start=False, stop=True), reads=[r_add, r_ident], writes=[psr[pi]])
            pt, r_pt, _ = ptr.next()
            pf, r_pf, _ = pfr.next()
            kb.op("act", lambda e, pi=pi, n=n, pf=pf: e.activation(
                out=pf[:n, 0:ncomp * nv], in_=ps_f32(pi)[:n, 0:ncomp * nv], func=AF.Exp),
                reads=[psr[pi]], writes=[r_pf])
            kb.op("pool", lambda e, n=n, pf=pf, pt=pt: e.tensor_copy(out=pt[:n, 0:ncomp * nv],
                                                                    in_=pf[:n, 0:ncomp * nv]),
                  reads=[r_pf], writes=[r_pt])
            for c in range(ncomp):
                kb.op("pe", lambda e, c=c, n=n, pt=pt, vblk=vblk, i=i: e.matmul(
                    ps_f32(obank[c])[:nv, 0:129], lhsT=pt[:n, c * nv:(c + 1) * nv], rhs=vblk,
                    start=(i == 0), stop=(i == len(ctx) - 1)), reads=[r_pt, r_vth], writes=[psr[obank[c]]])
        osb, r_os, _ = osr.next()
        kb.op("dve", lambda e: e.reciprocal(out=fs[:nv, 0:1], in_=ps_f32(4)[:nv, 128:129]), reads=[psr[4]],
              writes=[r_fs])
        kb.op("dve", lambda e: e.tensor_scalar(out=osb[:nv, :], in0=ps_f32(4)[:nv, 0:128], scalar1=fs[:nv, 0:1],
                                               scalar2=None, op0=ALU.mult), reads=[psr[4], r_fs], writes=[r_os])
        if mixer == "A":
            kb.op("dve", lambda e: e.reciprocal(out=fs[:nv, 1:2], in_=ps_f32(5)[:nv, 128:129]), reads=[psr[5]],
                  writes=[r_fs])
            kb.op("dve", lambda e: e.tensor_tensor(out=fs[:nv, 2:3], in0=fs[:nv, 1:2], in1=lsc[:nv, 5:6],
                                                   op=ALU.mult), reads=[r_fs, r_lam], writes=[r_fs])
            o2, r_o2, _ = o2r.next()
            kb.op("dve", lambda e: e.scalar_tensor_tensor(out=o2[:nv, :], in0=ps_f32(5)[:nv, 0:128],
                                                          scalar=fs[:nv, 2:3], in1=osb[:nv, :], op0=ALU.mult,
                                                          op1=ALU.add), reads=[psr[5], r_fs, r_os], writes=[r_o2])
            kb.op("act", lambda e: e.activation(out=osb[:nv, :], in_=o2[:nv, :], func=AF.Square,
                                                accum_out=fs[:nv, 3:4]), reads=[r_o2], writes=[r_os, r_fs])
            kb.op("act", lambda e: e.activation(out=fs[:nv, 4:5], in_=fs[:nv, 3:4], func=AF.Ln, bias=EPS,
                                                scale=1.0 / 128), reads=[r_fs], writes=[r_fs])
            kb.op("act", lambda e: e.activation(out=fs[:nv, 5:6], in_=fs[:nv, 4:5], func=AF.Exp, scale=-0.5),
                  reads=[r_fs], writes=[r_fs])
            kb.op("dve", lambda e: e.scalar_tensor_tensor(out=osb[:nv, :], in0=o2[:nv, :], scalar=fs[:nv, 5:6],
                                                          in1=gains["sub"][0][:nv, :], op0=ALU.mult, op1=ALU.mult),
                  reads=[r_o2, r_fs, gains["sub"][1]], writes=[r_os])
        zb, r_zb, _ = zbr.next()
        kb.op("act", lambda e: e.copy(out=zb[:nv, 0:128], in_=osb[:nv, :]), reads=[r_os], writes=[r_zb])
        od = oTA if mixer == "A" else oTB
        transpose_store(zb, r_zb, nv, 1, od[hq:hq + 1, :, cols].rearrange("h d t -> d h t"))

    import os
    NHA = int(os.environ.get("K_NHA", "8"))
    NHB = int(os.environ.get("K_NHB", "8"))
    TL = [int(x) for x in os.environ.get("K_TILES", ",".join(str(i) for i in range(NT))).split(",")]
    for h in range(NHA):
        if not os.environ.get("K_NOLOAD"):
            load_head("A", h)
        if not os.environ.get("K_NOPREP"):
            prep_head(h, qTA, h)
        for t in ([] if os.environ.get("K_NOATT") else TL):
            attend("A", h, h, t)
    for hbi in range(NHB):
        if hbi % 4 == 0:
            load_head("B", hbi // 4)
        prep_head(8 + hbi, qTB, hbi)
        for t in TL:
            attend("B", 8 + hbi, hbi, t)
    if os.environ.get("KDBG") == "2b":
        kb.emit()
        es.close()
        return nc
    kb.barrier()
    ar.release(m2b)
    ar.release(m2)
    m3 = ar.mark()
    woa = ar.get([8, 2048], BF16)
    wob = ar.get([8, 2048], BF16)
    wout = ar.get([16, 2048], BF16)
    r_w3 = kb.R("w3")
    stg3 = Ring("stg3", 2, [8, 256], F32)
    for (wsrc, wdst, nk) in ((w_o_a, woa, 8), (w_o_b, wob, 8), (w_out, wout, 16)):
        ncol = 256 if nk == 8 else 128
        for c0 in range(0, 2048, ncol):
            stg, r_stg, c_stg = stg3.next()
            sv = stg.rearrange("p a b -> p (a b)")[:, 0:nk * ncol].rearrange("p (a b) -> p a b", b=ncol)
            kb.op("sp", lambda e, sv=sv, wsrc=wsrc, c0=c0, ncol=ncol: e.dma_start(
                out=sv, in_=wsrc[:, c0:c0 + ncol].rearrange("(k p) n -> p k n", p=128)), writes=[r_stg], chan=c_stg)
            kb.op("pool", lambda e, sv=sv, wdst=wdst, c0=c0, ncol=ncol: e.tensor_copy(
                out=wdst[:, :, c0:c0 + ncol], in_=sv), reads=[r_stg], writes=[r_w3])
    oat = Ring("oat", 1, [8, 128], BF16)
    obt = Ring("obt", 1, [8, 128], BF16)
    gtr = Ring("gt", 2, [2, 512], F32)
    xtr = Ring("xt3", 1, [D], F32)
    mgr = Ring("mg", 2, [512], F32)
    mg2r = Ring("mg2", 2, [512], F32)
    mgbr = Ring("mgb", 1, [D], BF16)
    mTr = Ring("mT", 1, [16, 128], BF16)
    for t in range(NT):
        nv = tile_rows(t)
        cols = slice(t * 128, t * 128 + nv)
        oa_t, r_oa, c_oa = oat.next()
        ob_t, r_ob, c_ob = obt.next()
        xt3, r_xt3, c_xt3 = xtr.next()
        kb.op("sp", lambda e, oa_t=oa_t, cols=cols, nv=nv: e.dma_start(
            out=oa_t[:, :, :nv], in_=oTA[:, :, cols].rearrange("h d t -> d h t")), writes=[r_oa], chan=c_oa)
        kb.op("sp", lambda e, ob_t=ob_t, cols=cols, nv=nv: e.dma_start(
            out=ob_t[:, :, :nv], in_=oTB[:, :, cols].rearrange("h d t -> d h t")), writes=[r_ob], chan=c_ob)
        kb.op("sp", lambda e, xt3=xt3, t=t, nv=nv: e.dma_start(out=xt3[:nv, :], in_=x_rows(t)), writes=[r_xt3],
              chan=c_xt3)
        mgb, r_mgb, _ = mgbr.next()
        for cb in range(4):
            cs = slice(cb * 512, (cb + 1) * 512)
            pa, pb = cb % 2, 2 + cb % 2
            gt, r_gt, c_gt = gtr.next()
            for gi in range(2):
                kb.op("sp", lambda e, gt=gt, t=t, nv=nv, gi=gi, cb=cb: e.dma_start(
                    out=gt[:nv, gi, :], in_=gates[t * 128:t * 128 + nv, gi * 2048 + cb * 512:gi * 2048 + (cb + 1) * 512]),
                    writes=[r_gt], chan=c_gt)
            for k in range(8):
                kb.op("pe", lambda e, k=k, pa=pa, cs=cs, oa_t=oa_t, nv=nv: e.matmul(
                    ps_f32(pa)[:nv, :], lhsT=oa_t[:, k, :nv], rhs=woa[:, k, cs], start=(k == 0), stop=(k == 7)),
                    reads=[r_oa, r_w3], writes=[psr[pa]])
            for k in range(8):
                kb.op("pe", lambda e, k=k, pb=pb, cs=cs, ob_t=ob_t, nv=nv: e.matmul(
                    ps_f32(pb)[:nv, :], lhsT=ob_t[:, k, :nv], rhs=wob[:, k, cs], start=(k == 0), stop=(k == 7)),
                    reads=[r_ob, r_w3], writes=[psr[pb]])
            mg, r_mg, _ = mgr.next()
            mg2, r_mg2, _ = mg2r.next()
            kb.op("dve", lambda e, pa=pa, cs=cs, mg=mg, gt=gt, nv=nv: e.tensor_tensor(
                out=mg[:nv, :], in0=ps_f32(pa)[:nv, :], in1=gt[:nv, 0, :], op=ALU.mult),
                reads=[psr[pa], r_gt], writes=[r_mg])
            kb.op("dve", lambda e, pb=pb, cb=cb, mg2=mg2, gt=gt, nv=nv: e.tensor_tensor(
                out=mg2[:nv, :], in0=ps_f32(pb)[:nv, :], in1=gt[:nv, 1, :],
                op=ALU.mult), reads=[psr[pb], r_gt], writes=[r_mg2])
            kb.op("pool", lambda e, mg=mg, mg2=mg2, nv=nv: e.tensor_tensor(out=mg[:nv, :], in0=mg[:nv, :],
                                                                           in1=mg2[:nv, :], op=ALU.add),
                  reads=[r_mg, r_mg2], writes=[r_mg])
            kb.op("act", lambda e, mg=mg, mgb=mgb, cs=cs, nv=nv: e.copy(out=mgb[:nv, cs], in_=mg[:nv, :]),
                  reads=[r_mg], writes=[r_mgb])
        mT, r_mT, _ = mTr.next()
        for half in range(2):
            pi = 6 + half
            for kk in range(8):
                k = half * 8 + kk
                kb.op("pe", lambda e, k=k, kk=kk, pi=pi, mgb=mgb, nv=nv: e.transpose(
                    out=ps_bf16(pi)[:, kk * 128:kk * 128 + nv], in_=mgb[:nv, k * 128:(k + 1) * 128],
                    identity=ident[:nv, :nv]), reads=[r_mgb, r_ident], writes=[psr[pi]])
            kb.op("dve", lambda e, pi=pi, half=half, mT=mT, nv=nv: e.tensor_copy(
                out=mT[:, half * 8:(half + 1) * 8, :nv],
                in_=ps_bf16(pi).rearrange("p (a b) -> p a b", b=128)[:, :, :nv]), reads=[psr[pi]], writes=[r_mT])
        x1, r_x1, c_x1 = xt3, r_xt3, c_xt3
        for cb in range(4):
            cs = slice(cb * 512, (cb + 1) * 512)
            pi = 4 + cb % 2
            for k in range(16):
                kb.op("pe", lambda e, k=k, pi=pi, cs=cs, mT=mT, nv=nv: e.matmul(
                    ps_f32(pi)[:nv, :], lhsT=mT[:, k, :nv], rhs=wout[:, k, cs], start=(k == 0), stop=(k == 15)),
                    reads=[r_mT, r_w3], writes=[psr[pi]])
            kb.op("dve", lambda e, pi=pi, cs=cs, x1=x1, xt3=xt3, nv=nv: e.tensor_tensor(
                out=x1[:nv, cs], in0=ps_f32(pi)[:nv, :], in1=xt3[:nv, cs], op=ALU.add),
                reads=[psr[pi], r_xt3], writes=[r_xt3])
        kb.op("sp", lambda e, x1=x1, t=t, nv=nv: e.dma_start(out=x1s[t * 128:t * 128 + nv, :], in_=x1[:nv, :]),
              reads=[r_x1, r_dram], chan=c_x1)
    kb.barrier()
    ar.release(m3)
    GT = 6
    g2bc = ar.get([D], F32)
    r_g2 = kb.R("g2")
    kb.op("sp", lambda e: e.dma_start(out=g2bc, in_=norm2_g.partition_broadcast(128)), writes=[r_g2],
          chan=kb.chan("c1"))
    yacc = [ar.get([D], F32) for _ in range(GT)]
    r_y = [kb.R("y%d" % i) for i in range(GT)]
    h2T = ar.get([16, GT * 128], BF16)
    r_h2T = kb.R("h2T")
    hb3 = ar.get([D], BF16)
    r_hb3 = kb.R("hb3")
    junk3 = ar.get([D], BF16)
    r_junk3 = kb.R("junk3")
    ss3 = ar.get([8], F32)
    r_ss3 = kb.R("ss3")
    w1r = Ring("w1s", 2, [16, 256], BF16)
    w2r = Ring("w2s", 2, [2, 2048], BF16)
    s1r = Ring("s1", 1, [16, 256], F32)
    s2r = Ring("s2", 1, [2, 2048], F32)
    uTr = Ring("uT", 2, [2, GT * 128], BF16)
    rlr = Ring("rl", 2, [512], F32)
    NSUB = int(os.environ.get("K_NSUB", "32"))
    for g0 in range(0, NT, GT):
        tiles = list(range(g0, g0 + GT))
        for i, t in enumerate(tiles):
            nv = tile_rows(t)
            kb.op("sp", lambda e, i=i, t=t, nv=nv: e.dma_start(out=yacc[i][:nv, :], in_=x1s[t * 128:t * 128 + nv, :]),
                  writes=[r_y[i]], chan=kb.chan("yl%d" % i))
            kb.op("act", lambda e, i=i, nv=nv: e.activation(out=junk3[:nv, :], in_=yacc[i][:nv, :], func=AF.Square,
                                                          accum_out=ss3[:nv, 0:1]), reads=[r_y[i]],
                  writes=[r_junk3, r_ss3])
            kb.op("act", lambda e, nv=nv: e.activation(out=ss3[:nv, 1:2], in_=ss3[:nv, 0:1], func=AF.Ln, bias=EPS,
                                                     scale=1.0 / D), reads=[r_ss3], writes=[r_ss3])
            kb.op("act", lambda e, nv=nv: e.activation(out=ss3[:nv, 2:3], in_=ss3[:nv, 1:2], func=AF.Exp,
                                                     scale=-0.5), reads=[r_ss3], writes=[r_ss3])
            kb.op("dve", lambda e, i=i, nv=nv: e.scalar_tensor_tensor(
                out=hb3[:nv, :], in0=yacc[i][:nv, :], scalar=ss3[:nv, 2:3], in1=g2bc[:nv, :], op0=ALU.mult,
                op1=ALU.mult), reads=[r_y[i], r_ss3, r_g2], writes=[r_hb3])
            for half in range(2):
                pi = 6 + half
                for kk in range(8):
                    k = half * 8 + kk
                    kb.op("pe", lambda e, k=k, kk=kk, pi=pi, nv=nv: e.transpose(
                        out=ps_bf16(pi)[:, kk * 128:kk * 128 + nv], in_=hb3[:nv, k * 128:(k + 1) * 128],
                        identity=ident[:nv, :nv]), reads=[r_hb3, r_ident], writes=[psr[pi]])
                kb.op("dve", lambda e, pi=pi, half=half, i=i, nv=nv: e.tensor_copy(
                    out=h2T[:, half * 8:(half + 1) * 8, i * 128:i * 128 + nv],
                    in_=ps_bf16(pi).rearrange("p (a b) -> p a b", b=128)[:, :, :nv]), reads=[psr[pi]],
                    writes=[r_h2T])
            if nv < 128:
                kb.op("pool", lambda e, i=i, nv=nv: e.memset(h2T[:, :, i * 128 + nv:(i + 1) * 128], 0.0),
                      writes=[r_h2T])
        for sub in range(NSUB):
            f0 = sub * 256
            s1, r_s1, c_s1 = s1r.next()
            s2, r_s2, c_s2 = s2r.next()
            w1s, r_w1s, _ = w1r.next()
            w2s, r_w2s, _ = w2r.next()
            kb.op("sp", lambda e, s1=s1, f0=f0: e.dma_start(
                out=s1, in_=w_ff1[:, f0:f0 + 256].rearrange("(k p) n -> p k n", p=128)), writes=[r_s1], chan=c_s1)
            kb.op("sp", lambda e, s2=s2, f0=f0: e.dma_start(
                out=s2, in_=w_ff2[f0:f0 + 256, :].rearrange("(c p) n -> p c n", p=128)), writes=[r_s2], chan=c_s2)
            kb.op("pool", lambda e, s1=s1, w1s=w1s: e.tensor_copy(out=w1s, in_=s1), reads=[r_s1], writes=[r_w1s])
            kb.op("pool", lambda e, s2=s2, w2s=w2s: e.tensor_copy(out=w2s, in_=s2), reads=[r_s2], writes=[r_w2s])
            uT, r_uT, _ = uTr.next()
            for fc in range(2):
                for (c0, n) in ((0, 512), (512, GT * 128 - 512)):
                    pi = cnt["mm"] % 2
                    cnt["mm"] += 1
                    for k in range(16):
                        kb.op("pe", lambda e, k=k, pi=pi, fc=fc, c0=c0, n=n, w1s=w1s: e.matmul(
                            ps_f32(pi)[:, :n], lhsT=w1s[:, k, fc * 128:(fc + 1) * 128], rhs=h2T[:, k, c0:c0 + n],
                            start=(k == 0), stop=(k == 15)), reads=[r_w1s, r_h2T], writes=[psr[pi]])
                    rl, r_rl, _ = rlr.next()
                    kb.op("act", lambda e, pi=pi, n=n, rl=rl: e.activation(out=rl[:, :n], in_=ps_f32(pi)[:, :n],
                                                                         func=AF.Relu), reads=[psr[pi]],
                          writes=[r_rl])
                    kb.op("pool", lambda e, fc=fc, c0=c0, n=n, rl=rl, uT=uT: e.tensor_tensor(
                        out=uT[:, fc, c0:c0 + n], in0=rl[:, :n], in1=rl[:, :n], op=ALU.mult), reads=[r_rl],
                        writes=[r_uT])
            for i, t in enumerate(tiles):
                nv = tile_rows(t)
                for cb in range(4):
                    cs = slice(cb * 512, (cb + 1) * 512)
                    pi = 2 + cnt["tr"] % 4
                    cnt["tr"] += 1
                    for fc in range(2):
                        kb.op("pe", lambda e, fc=fc, pi=pi, i=i, nv=nv, cs=cs, uT=uT, w2s=w2s: e.matmul(
                            ps_f32(pi)[:nv, :], lhsT=uT[:, fc, i * 128:i * 128 + nv], rhs=w2s[:, fc, cs],
                            start=(fc == 0), stop=(fc == 1)), reads=[r_uT, r_w2s], writes=[psr[pi]])
                    kb.op("dve", lambda e, pi=pi, i=i, nv=nv, cs=cs: e.tensor_tensor(
                        out=yacc[i][:nv, cs], in0=ps_f32(pi)[:nv, :], in1=yacc[i][:nv, cs], op=ALU.add),
                        reads=[psr[pi], r_y[i]], writes=[r_y[i]])
        for i, t in enumerate(tiles):
            nv = tile_rows(t)
            dst = yp[t * 128:(t + 1) * 128, :] if t < NPT else ys[(t - NPT) * DEC:(t - NPT + 1) * DEC, :]
            kb.op("sp", lambda e, i=i, nv=nv, dst=dst: e.dma_start(out=dst, in_=yacc[i][:nv, :]), reads=[r_y[i]],
                  chan=kb.chan("yl%d" % i))
    kb.emit()
    es.close()
    return nc


def _t5_bucket(rel):
    import math
    half, exact = 16, 8
    n = np.abs(rel)
    nf = np.maximum(n, 1).astype(np.float32)
    large = exact + (np.log(nf / np.float32(exact)) / np.float32(math.log(128 / exact))
                     * np.float32(half - exact)).astype(np.int32)
    large = np.minimum(large, half - 1)
    return np.where(rel > 0, half, 0) + np.where(n < exact, n, large)


def _bias_layout(rel_bias):
    q = np.arange(128)[:, None]
    s_ = np.arange(128)[None, :]
    out = np.zeros((16, 2, 128, 128), np.float32)
    for ty, off in ((0, 0), (1, -128)):
        b = _t5_bucket(s_ - q + off)
        out[:, ty] = np.transpose(rel_bias[b], (2, 0, 1))
    return out


def _chunk_mask():
    m = np.zeros((128, 128), np.float32)
    m[0:64, 64:128] = NEG
    return m


def make_in_maps(inputs):
    f = lambda a: np.ascontiguousarray(np.asarray(a, dtype=np.float32))
    x_prompt = f(inputs["x_prompt"])[0]
    x_sample = f(inputs["x_sample"])
    xb = x_prompt.reshape(128, 128, D)
    shared = {
        "rel_bias": f(inputs["rel_bias"]),
        "norm1_g": f(inputs["norm1_g"])[0],
        "w_in": f(inputs["w_in"])[0],
        "qn_a_g": f(inputs["qn_a_g"])[0],
        "kn_a_g": f(inputs["kn_a_g"])[0],
        "lamv": np.stack([f(inputs[k])[0] for k in ("lam_q1", "lam_k1", "lam_q2", "lam_k2")], 0),
        "subln_a_g": f(inputs["subln_a_g"])[0],
        "qn_b_g": f(inputs["qn_b_g"])[0],
        "kn_b_g": f(inputs["kn_b_g"])[0],
        "w_o_a": f(inputs["w_o_a"])[0],
        "w_o_b": f(inputs["w_o_b"])[0],
        "w_out": f(inputs["w_out"])[0],
        "norm2_g": f(inputs["norm2_g"])[0],
        "w_ff1": f(inputs["w_ff1"])[0],
        "w_ff2": f(inputs["w_ff2"])[0],
        "c_ident": np.eye(128, dtype=np.float32),
        "c_oh": np.zeros((32, 512), np.float32),
        "c_cm": _chunk_mask(),
        "c_bn": _bias_layout(f(inputs["rel_bias"])),
    }
    maps = []
    for c in range(NCORES):
        m = dict(shared)
        for nm in ("w_in", "w_o_a", "w_o_b", "w_out", "w_ff1", "w_ff2"):
            a = shared[nm]
            m[nm] = np.concatenate([a, np.full((1, a.shape[1]), float(c), np.float32)], 0)
        m["xp"] = np.ascontiguousarray(xb[c::8].reshape(NPT * 128, D))
        m["xs"] = np.ascontiguousarray(x_sample[2 * c:2 * c + 2].reshape(NST * DEC, D))
        m["cak"] = np.ascontiguousarray(f(inputs["cache_a_k"])[0, 2 * c:2 * c + 2].reshape(NST, PAST, 1024))
        m["cav"] = np.ascontiguousarray(f(inputs["cache_a_v"])[0, 2 * c:2 * c + 2].reshape(NST, PAST, 1024))
        m["cbk"] = np.ascontiguousarray(f(inputs["cache_b_k"])[0, 2 * c:2 * c + 2].reshape(NST, PAST, 256))
        m["cbv"] = np.ascontiguousarray(f(inputs["cache_b_v"])[0, 2 * c:2 * c + 2].reshape(NST, PAST, 256))
        m["cbi"] = np.ascontiguousarray(f(inputs["cache_b_kidx"])[0, 2 * c:2 * c + 2].reshape(NST, PAST, 64))
        sel = np.zeros(27, np.float32)
        for r in range(8):
            sel[r] = 1.0 if r == c - 1 else 0.0
            sel[9 + r] = 1.0 if r == c else 0.0
            sel[18 + r] = 1.0 if r > c else 0.0
        sel[8] = 1.0 if c == 0 else 0.0
        m["c_sel"] = sel
        maps.append(m)
    return maps


def assemble(results):
    def prompt(name, w):
        out = np.zeros((128, 128, w), np.float32)
        for c in range(NCORES):
            out[c::8] = results[c][name].reshape(NPT, 128, w)
        return out.reshape(SEQ, w)

    def sample(name, w):
        return np.concatenate([results[c][name].reshape(NST, DEC, w) for c in range(NCORES)], 0)

    y_prompt = prompt("yp", D).reshape(1, SEQ, D)
    y_sample = sample("ys", D)
    p_a_k = prompt("pak", 1024).reshape(1, 1, SEQ, 8, 128)
    p_a_v = prompt("pav", 1024).reshape(1, 1, SEQ, 8, 128)
    p_b_k = prompt("pbk", 256).reshape(1, 1, SEQ, 2, 128)
    p_b_v = prompt("pbv", 256).reshape(1, 1, SEQ, 2, 128)
    p_b_i = prompt("pbi", 64).reshape(1, 1, SEQ, 64)
    s_a_k = sample("sak", 1024).reshape(1, 16, DEC, 8, 128)
    s_a_v = sample("sav", 1024).reshape(1, 16, DEC, 8, 128)
    s_b_k = sample("sbk", 256).reshape(1, 16, DEC, 2, 128)
    s_b_v = sample("sbv", 256).reshape(1, 16, DEC, 2, 128)
    s_b_i = sample("sbi", 64).reshape(1, 16, DEC, 64)
    return (y_prompt, y_sample, p_a_k, p_a_v, p_b_k, p_b_v, p_b_i, s_a_k, s_a_v, s_b_k, s_b_v, s_b_i)


_NC_CACHE = {}


def kernel(**inputs):
    if "nc" not in _NC_CACHE:
        import os
        _NC_CACHE["nc"] = build_program(stop_after=int(os.environ.get("KSTOP", "99")))
    nc = _NC_CACHE["nc"]
    in_maps = make_in_maps(inputs)
    res = run_bass_kernel_spmd(nc, in_maps, core_ids=list(range(NCORES)))
    return assemble(res.results)
```

```python
import numpy as np
import concourse.bass as bass
import concourse.mybir as mybir
from concourse.bass_utils import run_bass_kernel_spmd

F32 = mybir.dt.float32
BF16 = mybir.dt.bfloat16
ALU = mybir.AluOpType
AF = mybir.ActivationFunctionType
AX = mybir.AxisListType

NCORES = 8
D = 2048
SEQ = 16384
NPT = 16
NST = 2
NT = NPT + NST
TOK = NT * 128
PAST = 1024
DEC = 32
H_A = 8
H_B = 8
HKV = 2
H_IDX = 16
D_IDX = 64
TOPK = 256
D_FF = 8192
EPS = 1e-6
IN_COLS = 9808
NEG = -30000.0
BIGNEG = -1.0e30

R_KTA = 0
R_VA = 1024
R_KTB = 2048
R_VB = 2304
R_KIT = 2560
NROWS = 2688
SK = 1152

C_QA, C_KA, C_VA, C_QB, C_KB, C_VB, C_QI, C_KI, C_WI, C_GA, C_GB = (
    0, 1024, 2048, 3072, 4096, 4352, 4608, 5632, 5696, 5712, 7760)


class Res:
    __slots__ = ("w", "r", "name")

    def __init__(self, name=""):
        self.w = None
        self.r = {}
        self.name = name


class KB:
    ENGS = ("pe", "act", "dve", "pool", "sp")

    def __init__(self, nc):
        self.nc = nc
        self.ops = {e: [] for e in self.ENGS}
        self.count = {e: 0 for e in self.ENGS}
        self.waited = {e: {} for e in self.ENGS}
        self.pending = {e: {} for e in self.ENGS}
        self.chans = {}
        self.res = []

    def R(self, name=""):
        r = Res(name)
        self.res.append(r)
        return r

    def chan(self, name):
        if name not in self.chans:
            self.chans[name] = 0
        return name

    def op(self, eng, fn, reads=(), writes=(), chan=None, cinc=16):
        need = dict(self.pending[eng])
        self.pending[eng] = {}

        def add(k, v):
            if need.get(k, 0) < v:
                need[k] = v

        for r in reads:
            if r.w is not None:
                add(*r.w)
        for w in writes:
            if w.w is not None:
                add(*w.w)
            for k, v in w.r.items():
                add(k, v)
        if chan is not None and self.chans[chan] > 0:
            add("c:" + chan, self.chans[chan])
        waits = []
        wd = self.waited[eng]
        for k, v in need.items():
            if eng == "pe" and k == "e:pe":
                continue
            if wd.get(k, 0) >= v:
                continue
            wd[k] = v
            waits.append((k, v))
        if chan is not None:
            self.chans[chan] += cinc
            tok = ("c:" + chan, self.chans[chan])
            inc = cinc
        else:
            self.count[eng] += 1
            tok = ("e:" + eng, self.count[eng])
            inc = 1
        self.ops[eng].append((fn, waits, tok[0], inc))
        for r in reads:
            if r.r.get(tok[0], 0) < tok[1]:
                r.r[tok[0]] = tok[1]
        for w in writes:
            w.w = tok
            w.r = {}
        return tok

    def barrier(self):
        toks = {}
        for e in self.ENGS:
            if self.count[e] > 0:
                toks["e:" + e] = self.count[e]
        for c, v in self.chans.items():
            if v > 0:
                toks["c:" + c] = v
        for e in self.ENGS:
            for k, v in toks.items():
                if self.pending[e].get(k, 0) < v:
                    self.pending[e][k] = v
        for r in self.res:
            r.w = None
            r.r = {}

    def emit(self):
        nc = self.nc
        import contextlib
        keys = ["e:" + e for e in self.ENGS] + ["c:" + c for c in self.chans]
        with contextlib.ExitStack() as es:
            sems = {}
            for i, k in enumerate(keys):
                sems[k] = es.enter_context(nc.semaphore("s%d" % i))
            block = es.enter_context(nc.Block())

            def run(eng, e):
                for fn, waits, tk, inc in self.ops[eng]:
                    for k, v in waits:
                        e.wait_ge(sems[k], v)
                    ins = fn(e)
                    ins.then_inc(sems[tk], inc)
                if eng == "sp":
                    for k in keys:
                        v = self.count[k[2:]] if k.startswith("e:") else self.chans[k[2:]]
                        if v > 0 and k != "e:sp":
                            e.wait_ge(sems[k], v)

            @block.tensor
            def _(e):
                run("pe", e)

            @block.scalar
            def _(e):
                run("act", e)

            @block.vector
            def _(e):
                run("dve", e)

            @block.gpsimd
            def _(e):
                run("pool", e)

            @block.sync
            def _(e):
                run("sp", e)


class Arena:
    def __init__(self, ap_bf16, nelem):
        self.ap = ap_bf16
        self.n = nelem
        self.off = 0

    def mark(self):
        return self.off

    def release(self, m):
        self.off = m

    def get(self, shape, dtype):
        n = 1
        for s in shape:
            n *= s
        nb = n * (4 if dtype == F32 else 2)
        nb = (nb + 63) // 64 * 64
        e0 = self.off // 2
        self.off += nb
        assert self.off // 2 <= self.n, "arena overflow %d" % self.off
        v = self.ap[:, e0:e0 + nb // 2]
        if dtype == F32:
            v = v.bitcast(F32)
        v = v[:, 0:n]
        if len(shape) == 2:
            v = v.rearrange("p (a b) -> p a b", b=shape[1])
        elif len(shape) == 3:
            v = v.rearrange("p (a b c) -> p a b c", b=shape[1], c=shape[2])
        return v


def build_program(stop_after=99):
    import os
    nc = bass.Bass("TRN2", target_bir_lowering=False)
    kb = KB(nc)

    def din(name, shape, dt=F32):
        return nc.dram_tensor(name, list(shape), dt, kind="ExternalInput").ap()

    def dout(name, shape, dt=F32):
        return nc.dram_tensor(name, list(shape), dt, kind="ExternalOutput").ap()

    xp = din("xp", [NPT * 128, D])
    xs = din("xs", [NST * DEC, D])
    cak = din("cak", [NST, PAST, 1024])
    cav = din("cav", [NST, PAST, 1024])
    cbk = din("cbk", [NST, PAST, 256])
    cbv = din("cbv", [NST, PAST, 256])
    cbi = din("cbi", [NST, PAST, 64])
    rel_bias = din("rel_bias", [32, 16])
    norm1_g = din("norm1_g", [D])
    w_in = din("w_in", [D + 1, IN_COLS])[0:D, :]
    qn_a_g = din("qn_a_g", [128])
    kn_a_g = din("kn_a_g", [128])
    lamv = din("lamv", [4, 64])
    subln_g = din("subln_a_g", [128])
    qn_b_g = din("qn_b_g", [128])
    kn_b_g = din("kn_b_g", [128])
    w_o_a = din("w_o_a", [1024 + 1, D])[0:1024, :]
    w_o_b = din("w_o_b", [1024 + 1, D])[0:1024, :]
    w_out = din("w_out", [D + 1, D])[0:D, :]
    norm2_g = din("norm2_g", [D])
    w_ff1 = din("w_ff1", [D + 1, D_FF])[0:D, :]
    w_ff2 = din("w_ff2", [D_FF + 1, D])[0:D_FF, :]
    c_ident = din("c_ident", [128, 128])
    c_sel = din("c_sel", [27])
    c_oh = din("c_oh", [32, 512])
    c_cm = din("c_cm", [128, 128])
    c_bn = din("c_bn", [16, 2, 128, 128])

    yp = dout("yp", [NPT * 128, D])
    ys = dout("ys", [NST * DEC, D])
    pak = dout("pak", [NPT * 128, 1024])
    pav = dout("pav", [NPT * 128, 1024])
    pbk = dout("pbk", [NPT * 128, 256])
    pbv = dout("pbv", [NPT * 128, 256])
    pbi = dout("pbi", [NPT * 128, 64])
    sak = dout("sak", [NST * DEC, 1024])
    sav = dout("sav", [NST * DEC, 1024])
    sbk = dout("sbk", [NST * DEC, 256])
    sbv = dout("sbv", [NST * DEC, 256])
    sbi = dout("sbi", [NST * DEC, 64])

    kvl = nc.dram_tensor("kvl", [NROWS, 2048], BF16)
    kva = nc.dram_tensor("kva", [NCORES * NROWS, 2048], BF16)
    kvs = nc.dram_tensor("kvs", [NST, NROWS, SK], BF16).ap()
    qTA = nc.dram_tensor("qTA", [8, 128, TOK], BF16).ap()
    qTB = nc.dram_tensor("qTB", [8, 128, TOK], BF16).ap()
    qTI = nc.dram_tensor("qTI", [8, 128, TOK], BF16).ap()
    gates = nc.dram_tensor("gates", [TOK, 4096], F32).ap()
    kvl_ap = kvl.ap()

    import contextlib
    es = contextlib.ExitStack()
    ARENA_ELEMS = 100 * 1024
    arena_t = es.enter_context(nc.sbuf_tensor("arena", [128, ARENA_ELEMS], BF16))
    ar = Arena(arena_t[:], ARENA_ELEMS)
    psf = [es.enter_context(nc.psum_tensor("ps%d" % i, [128, 512], F32)) for i in range(8)]
    psr = [kb.R("ps%d" % i) for i in range(8)]

    def ps_f32(i):
        return psf[i][:]

    def ps_bf16(i):
        return psf[i][:].bitcast(BF16)

    def tile_rows(t):
        return 128 if t < NPT else DEC

    def x_rows(t):
        if t < NPT:
            return xp[t * 128:(t + 1) * 128, :]
        s = t - NPT
        return xs[s * DEC:(s + 1) * DEC, :]

    ident = ar.get([128], BF16)
    r_ident = kb.R("ident")
    kb.op("pool", lambda e: e.dma_start(out=ident, in_=c_ident), writes=[r_ident], chan=kb.chan("c0"))
    g1bc = ar.get([D], F32)
    r_g1 = kb.R("g1")
    kb.op("sp", lambda e: e.dma_start(out=g1bc, in_=norm1_g.partition_broadcast(128)), writes=[r_g1],
          chan=kb.chan("c1"))
    gains = {}
    for nm, src in (("qa", qn_a_g), ("ka", kn_a_g), ("qb", qn_b_g), ("kb", kn_b_g), ("sub", subln_g)):
        tl = ar.get([128], F32)
        rr = kb.R("gain" + nm)
        kb.op("sp", (lambda tl, src: lambda e: e.dma_start(out=tl, in_=src.partition_broadcast(128)))(tl, src),
              writes=[rr], chan=kb.chan("c1"))
        gains[nm] = (tl, rr)
    wiT = ar.get([NT, 16], F32)
    r_wi = [kb.R("wi%d" % t) for t in range(NT)]

    m0 = ar.mark()
    hT = ar.get([16, TOK], BF16)
    r_hT = [kb.R("hT%d" % t) for t in range(NT)]
    m1 = ar.mark()
    xt = [ar.get([D], F32) for _ in range(2)]
    r_xt = [kb.R("xt%d" % i) for i in range(2)]
    hb = [ar.get([D], BF16) for _ in range(2)]
    r_hb = [kb.R("hb%d" % i) for i in range(2)]
    junk = ar.get([D], BF16)
    r_junk = kb.R("junk")
    ss = ar.get([8], F32)
    r_ss = kb.R("ss")

    def load_x(t):
        b = t % 2
        nv = tile_rows(t)
        kb.op("sp", lambda e: e.dma_start(out=xt[b][:nv, :], in_=x_rows(t)), writes=[r_xt[b]],
              chan=kb.chan("xl%d" % b))

    load_x(0)
    for t in range(NT):
        b = t % 2
        nv = tile_rows(t)
        if t + 1 < NT:
            load_x(t + 1)
        kb.op("act", lambda e, b=b, nv=nv: e.activation(out=junk[:nv, :], in_=xt[b][:nv, :], func=AF.Square,
                                                      accum_out=ss[:nv, 0:1]),
              reads=[r_xt[b]], writes=[r_junk, r_ss])
        kb.op("act", lambda e, nv=nv: e.activation(out=ss[:nv, 1:2], in_=ss[:nv, 0:1], func=AF.Ln,
                                                 bias=EPS, scale=1.0 / D), reads=[r_ss], writes=[r_ss])
        kb.op("act", lambda e, nv=nv: e.activation(out=ss[:nv, 2:3], in_=ss[:nv, 1:2], func=AF.Exp,
                                                 scale=-0.5), reads=[r_ss], writes=[r_ss])
        kb.op("dve", lambda e, b=b, nv=nv: e.scalar_tensor_tensor(out=hb[b][:nv, :], in0=xt[b][:nv, :],
                                                                scalar=ss[:nv, 2:3], in1=g1bc[:nv, :],
                                                                op0=ALU.mult, op1=ALU.mult),
              reads=[r_xt[b], r_ss, r_g1], writes=[r_hb[b]])
        for half in range(2):
            pi = 6 + half
            for kk in range(8):
                k = half * 8 + kk
                kb.op("pe", lambda e, b=b, nv=nv, k=k, kk=kk, pi=pi: e.transpose(
                    out=ps_bf16(pi)[:, kk * 128:kk * 128 + nv], in_=hb[b][:nv, k * 128:(k + 1) * 128],
                    identity=ident[:nv, :nv]),
                    reads=[r_hb[b], r_ident], writes=[psr[pi]])
            src = ps_bf16(pi).rearrange("p (a b) -> p a b", b=128)[:, :, :nv]
            dst = hT[:, half * 8:(half + 1) * 8, t * 128:t * 128 + nv]
            eng = "act" if half == 0 else "dve"
            if eng == "act":
                kb.op("act", lambda e, src=src, dst=dst: e.copy(out=dst, in_=src), reads=[psr[pi]], writes=[r_hT[t]])
            else:
                kb.op("dve", lambda e, src=src, dst=dst: e.tensor_copy(out=dst, in_=src), reads=[psr[pi]],
                      writes=[r_hT[t]])

    if stop_after <= 0:
        dbg = ar.get([D], F32)
        r_dbg = kb.R("dbg")
        kb.op("dve", lambda e: e.tensor_copy(out=dbg.rearrange("p (a b) -> p a b", b=128),
                                             in_=hT[:, :, 0:128]), reads=[r_hT[0]], writes=[r_dbg])
        kb.op("sp", lambda e: e.dma_start(out=yp[0:128, :], in_=dbg), reads=[r_dbg], chan=kb.chan("dbg"))
        kb.emit()
        es.close()
        return nc

    ar.release(m1)
    kb.barrier()

    class Ring:
        def __init__(self, name, n, shape, dtype):
            self.t = [ar.get(shape, dtype) for _ in range(n)]
            self.r = [kb.R("%s%d" % (name, i)) for i in range(n)]
            self.c = [kb.chan("%s%d" % (name, i)) for i in range(n)]
            self.i = -1
            self.n = n

        def next(self):
            self.i = (self.i + 1) % self.n
            return self.t[self.i], self.r[self.i], self.c[self.i]

    kb.op("dve", lambda e: e.tensor_scalar(out=gains["qa"][0], in0=gains["qa"][0], scalar1=0.125, scalar2=None,
                                           op0=ALU.mult), reads=[gains["qa"][1]], writes=[gains["qa"][1]])
    kb.op("dve", lambda e: e.tensor_scalar(out=gains["qb"][0], in0=gains["qb"][0], scalar1=float(128 ** -0.5),
                                           scalar2=None, op0=ALU.mult), reads=[gains["qb"][1]],
          writes=[gains["qb"][1]])

    wring = Ring("wb", 2, [16, 512], BF16)
    sqr = Ring("sq", 2, [512], F32)
    znr = Ring("zn", 3, [512], F32)
    zbr = Ring("zb", 3, [512], BF16)
    trr = Ring("tr", 3, [4, 128], BF16)
    st8 = ar.get([64], F32)
    r_st8 = kb.R("st8")
    r_dram = kb.R("dram_scratch")
    mm_banks = [0, 1, 2]
    tr_banks = [6, 7]
    cnt = {"mm": 0, "tr": 0}

    def transpose_store(src, r_src, nv, nch, dst_ap, dst_res=r_dram):
        pi = tr_banks[cnt["tr"] % 2]
        cnt["tr"] += 1
        for c in range(nch):
            kb.op("pe", lambda e, c=c, pi=pi: e.transpose(out=ps_bf16(pi)[:, c * 128:c * 128 + nv],
                                                          in_=src[:nv, c * 128:(c + 1) * 128],
                                                          identity=ident[:nv, :nv]),
                  reads=[r_src, r_ident], writes=[psr[pi]])
        tb, r_tb, c_tb = trr.next()
        kb.op("dve", lambda e, pi=pi, tb=tb: e.tensor_copy(
            out=tb[:, 0:nch, 0:nv], in_=ps_bf16(pi).rearrange("p (a b) -> p a b", b=128)[:, 0:nch, 0:nv]),
            reads=[psr[pi]], writes=[r_tb])
        kb.op("sp", lambda e, tb=tb: e.dma_start(out=dst_ap, in_=tb[:, 0:nch, 0:nv]), reads=[r_tb, dst_res],
              chan=c_tb)

    def group_norm(pi, nv, ncols, gs, gain, r_gain):
        ng = ncols // gs
        sq, r_sq, _ = sqr.next()
        kb.op("act", lambda e: e.activation(out=sq[:nv, :ncols], in_=ps_f32(pi)[:nv, :ncols], func=AF.Square),
              reads=[psr[pi]], writes=[r_sq])
        kb.op("dve", lambda e: e.tensor_reduce(out=st8[:nv, 0:ng],
                                               in_=sq[:nv, :ncols].rearrange("p (g d) -> p g d", d=gs),
                                               axis=AX.X, op=ALU.add), reads=[r_sq], writes=[r_st8])
        kb.op("act", lambda e: e.activation(out=st8[:nv, 16:16 + ng], in_=st8[:nv, 0:ng], func=AF.Ln, bias=EPS,
                                            scale=1.0 / gs), reads=[r_st8], writes=[r_st8])
        kb.op("act", lambda e: e.activation(out=st8[:nv, 32:32 + ng], in_=st8[:nv, 16:16 + ng], func=AF.Exp,
                                            scale=-0.5), reads=[r_st8], writes=[r_st8])
        zn, r_zn, c_zn = znr.next()
        kb.op("dve", lambda e: e.tensor_tensor(
            out=zn[:nv, :ncols].rearrange("p (g d) -> p g d", d=gs),
            in0=ps_f32(pi)[:nv, :ncols].rearrange("p (g d) -> p g d", d=gs),
            in1=st8[:nv, 32:32 + ng].unsqueeze(2).to_broadcast([nv, ng, gs]), op=ALU.mult),
            reads=[psr[pi], r_st8], writes=[r_zn])
        nrep = ncols // 128
        kb.op("pool", lambda e: e.tensor_tensor(
            out=zn[:nv, :ncols].rearrange("p (g d) -> p g d", d=128),
            in0=zn[:nv, :ncols].rearrange("p (g d) -> p g d", d=128),
            in1=gain[:nv, :].unsqueeze(1).to_broadcast([nv, nrep, 128]), op=ALU.mult),
            reads=[r_zn, r_gain], writes=[r_zn])
        return zn, r_zn, c_zn

    def to_bf16(src, r_src, nv, ncols):
        zb, r_zb, c_zb = zbr.next()
        kb.op("act", lambda e: e.copy(out=zb[:nv, :ncols], in_=src[:nv, :ncols]), reads=[r_src], writes=[r_zb])
        return zb, r_zb, c_zb

    def out_rows(t, pten, sten, c0, ncols):
        if t < NPT:
            return pten[t * 128:(t + 1) * 128, c0:c0 + ncols]
        s = t - NPT
        return sten[s * DEC:(s + 1) * DEC, c0:c0 + ncols]

    def kt_dst(t, row0, nch):
        if t < NPT:
            return kvl_ap[row0:row0 + nch * 128, t * 128:(t + 1) * 128].rearrange("(h d) t -> d h t", d=128)
        s = t - NPT
        return kvs[s, row0:row0 + nch * 128, PAST:PAST + DEC].rearrange("(h d) t -> d h t", d=128)

    def v_dst(t, row0, nh):
        if t < NPT:
            return kvl_ap[row0:row0 + nh * 128, t * 128:(t + 1) * 128].rearrange("(h p) d -> p h d", p=128)
        s = t - NPT
        return kvs[s, row0:row0 + nh * 128, PAST:PAST + 128].rearrange("(h p) d -> p h d", p=128)[0:DEC]

    def evac(kind, b, t, pi, ncols):
        nv = tile_rows(t)
        cols = slice(t * 128, t * 128 + nv)
        if kind in ("qA", "qB"):
            g = gains["qa"] if kind == "qA" else gains["qb"]
            zn, r_zn, _ = group_norm(pi, nv, 512, 64 if kind == "qA" else 128, g[0], g[1])
            zb, r_zb, _ = to_bf16(zn, r_zn, nv, 512)
            qd = qTA if kind == "qA" else qTB
            transpose_store(zb, r_zb, nv, 4, qd[4 * b:4 * b + 4, :, cols].rearrange("h d t -> d h t"))
        elif kind == "kA":
            zn, r_zn, c_zn = group_norm(pi, nv, 512, 64, gains["ka"][0], gains["ka"][1])
            kb.op("sp", lambda e: e.dma_start(out=out_rows(t, pak, sak, b * 512, 512), in_=zn[:nv, :512]),
                  reads=[r_zn], chan=c_zn)
            zb, r_zb, _ = to_bf16(zn, r_zn, nv, 512)
            transpose_store(zb, r_zb, nv, 4, kt_dst(t, R_KTA + 4 * b * 128, 4))
        elif kind == "vA":
            zn, r_zn, c_zn = znr.next()
            kb.op("dve", lambda e: e.tensor_copy(out=zn[:nv, :512], in_=ps_f32(pi)[:nv, :512]), reads=[psr[pi]],
                  writes=[r_zn])
            kb.op("sp", lambda e: e.dma_start(out=out_rows(t, pav, sav, b * 512, 512), in_=zn[:nv, :512]),
                  reads=[r_zn], chan=c_zn)
            zb, r_zb, c_zb = to_bf16(zn, r_zn, nv, 512)
            kb.op("sp", lambda e: e.dma_start(out=v_dst(t, R_VA + 4 * b * 128, 4),
                                              in_=zb[:nv, :512].rearrange("p (h d) -> p h d", d=128)),
                  reads=[r_zb, r_dram], chan=c_zb)
        elif kind == "kvB":
            zn, r_zn, c_zn = group_norm(pi, nv, 256, 128, gains["kb"][0], gains["kb"][1])
            kb.op("dve", lambda e: e.tensor_copy(out=zn[:nv, 256:512], in_=ps_f32(pi)[:nv, 256:512]),
                  reads=[psr[pi]], writes=[r_zn])
            kb.op("sp", lambda e: e.dma_start(out=out_rows(t, pbk, sbk, 0, 256), in_=zn[:nv, 0:256]),
                  reads=[r_zn], chan=c_zn)
            kb.op("sp", lambda e: e.dma_start(out=out_rows(t, pbv, sbv, 0, 256), in_=zn[:nv, 256:512]),
                  reads=[r_zn], chan=c_zn)
            zb, r_zb, c_zb = to_bf16(zn, r_zn, nv, 512)
            kb.op("sp", lambda e: e.dma_start(out=v_dst(t, R_VB, 2),
                                              in_=zb[:nv, 256:512].rearrange("p (h d) -> p h d", d=128)),
                  reads=[r_zb, r_dram], chan=c_zb)
            transpose_store(zb, r_zb, nv, 2, kt_dst(t, R_KTB, 2))
        elif kind == "qi":
            zn, r_zn, c_zn = znr.next()
            kb.op("dve", lambda e: e.tensor_copy(out=zn[:nv, :512], in_=ps_f32(pi)[:nv, :512]), reads=[psr[pi]],
                  writes=[r_zn])
            zb, r_zb, _ = to_bf16(zn, r_zn, nv, 512)
            transpose_store(zb, r_zb, nv, 4, qTI[4 * b:4 * b + 4, :, cols].rearrange("h d t -> d h t"))
        elif kind == "kiwi":
            zn, r_zn, c_zn = znr.next()
            kb.op("dve", lambda e: e.tensor_copy(out=zn[:nv, 0:80], in_=ps_f32(pi)[:nv, 0:80]), reads=[psr[pi]],
                  writes=[r_zn])
            kb.op("sp", lambda e: e.dma_start(out=out_rows(t, pbi, sbi, 0, 64), in_=zn[:nv, 0:64]),
                  reads=[r_zn], chan=c_zn)
            kb.op("pool", lambda e: e.tensor_copy(out=wiT[:nv, t, :], in_=zn[:nv, 64:80]), reads=[r_zn],
                  writes=[r_wi[t]])
            zb, r_zb, _ = zbr.next()
            kb.op("act", lambda e: e.copy(out=zb[:nv, 0:64], in_=zn[:nv, 0:64]), reads=[r_zn], writes=[r_zb])
            kb.op("act", lambda e: e.copy(out=zb[:nv, 64:128], in_=zn[:nv, 0:64]), reads=[r_zn], writes=[r_zb])
            transpose_store(zb, r_zb, nv, 1, kt_dst(t, R_KIT, 1))
        elif kind in ("ga", "gb"):
            zn, r_zn, c_zn = znr.next()
            kb.op("act", lambda e: e.activation(out=zn[:nv, :512], in_=ps_f32(pi)[:nv, :512], func=AF.Exp,
                                                scale=-1.0), reads=[psr[pi]], writes=[r_zn])
            kb.op("dve", lambda e: e.tensor_scalar(out=zn[:nv, :512], in0=zn[:nv, :512], scalar1=1.0, scalar2=None,
                                                   op0=ALU.add), reads=[r_zn], writes=[r_zn])
            kb.op("dve", lambda e: e.reciprocal(out=zn[:nv, :512], in_=zn[:nv, :512]), reads=[r_zn], writes=[r_zn])
            off = (0 if kind == "ga" else 2048) + b * 512
            kb.op("sp", lambda e: e.dma_start(out=gates[t * 128:t * 128 + nv, off:off + 512], in_=zn[:nv, :512]),
                  reads=[r_zn, r_dram], chan=c_zn)

    def cache_conv():
        cbr = Ring("cb", 2, [1024], BF16)
        import os
        for s in range(NST if (stop_after != 1 or os.environ.get('KB_CC')) else 0):
            for jb in range(PAST // 128):
                rows = slice(jb * 128, (jb + 1) * 128)
                kc = slice(jb * 128, (jb + 1) * 128)
                cb, r_cb, c_cb = cbr.next()
                kb.op("pool", lambda e, cb=cb, s=s, rows=rows: e.dma_start(out=cb[:, :], in_=cak[s, rows, :]),
                      writes=[r_cb], chan=c_cb)
                for hh in range(2):
                    transpose_store(cb[:, hh * 512:(hh + 1) * 512], r_cb, 128, 4,
                                    kvs[s, R_KTA + hh * 512:R_KTA + (hh + 1) * 512, kc].rearrange(
                                        "(h d) t -> d h t", d=128))
                cb, r_cb, c_cb = cbr.next()
                kb.op("pool", lambda e, cb=cb, s=s, rows=rows: e.dma_start(out=cb[:, :], in_=cav[s, rows, :]),
                      writes=[r_cb], chan=c_cb)
                kb.op("sp", lambda e, cb=cb, s=s, kc=kc: e.dma_start(
                    out=kvs[s, R_VA:R_VA + 1024, kc].rearrange("(h p) d -> p h d", p=128),
                    in_=cb[:, :].rearrange("p (h d) -> p h d", d=128)), reads=[r_cb, r_dram], chan=c_cb)
                cb, r_cb, c_cb = cbr.next()
                kb.op("pool", lambda e, cb=cb, s=s, rows=rows: e.dma_start(out=cb[:, 0:256], in_=cbk[s, rows, :]),
                      writes=[r_cb], chan=c_cb)
                kb.op("pool", lambda e, cb=cb, s=s, rows=rows: e.dma_start(out=cb[:, 256:512], in_=cbv[s, rows, :]),
                      writes=[r_cb], chan=c_cb)
                kb.op("pool", lambda e, cb=cb, s=s, rows=rows: e.dma_start(out=cb[:, 512:576], in_=cbi[s, rows, :]),
                      writes=[r_cb], chan=c_cb)
                kb.op("pool", lambda e, cb=cb, s=s, rows=rows: e.dma_start(out=cb[:, 576:640], in_=cbi[s, rows, :]),
                      writes=[r_cb], chan=c_cb)
                transpose_store(cb[:, 0:256], r_cb, 128, 2,
                                kvs[s, R_KTB:R_KTB + 256, kc].rearrange("(h d) t -> d h t", d=128))
                kb.op("sp", lambda e, cb=cb, s=s, kc=kc: e.dma_start(
                    out=kvs[s, R_VB:R_VB + 256, kc].rearrange("(h p) d -> p h d", p=128),
                    in_=cb[:, 256:512].rearrange("p (h d) -> p h d", d=128)), reads=[r_cb, r_dram], chan=c_cb)
                transpose_store(cb[:, 512:640], r_cb, 128, 1,
                                kvs[s, R_KIT:R_KIT + 128, kc].rearrange("(h d) t -> d h t", d=128))

    blocks = [("kA", 0, C_KA, 512), ("kA", 1, C_KA + 512, 512), ("vA", 0, C_VA, 512), ("vA", 1, C_VA + 512, 512),
              ("kvB", 0, C_KB, 512), ("kiwi", 0, C_KI, 80), ("GATHER", 0, 0, 0),
              ("qA", 0, C_QA, 512), ("qA", 1, C_QA + 512, 512), ("qB", 0, C_QB, 512), ("qB", 1, C_QB + 512, 512),
              ("qi", 0, C_QI, 512), ("qi", 1, C_QI + 512, 512)]
    for b in range(4):
        blocks.append(("ga", b, C_GA + b * 512, 512))
    for b in range(4):
        blocks.append(("gb", b, C_GB + b * 512, 512))
    if stop_after == 1:
        import os
        blocks = [blocks[int(i)] for i in os.environ.get('KB_BL', '0,1,2,3,4,5,6').split(',')]
    r_kva = kb.R("kva")
    post_gather = [False]
    stgr = Ring("stg", 2, [16, 256], F32)
    for kind, b, c0, ncols in blocks:
        if kind == "GATHER":
            cache_conv()
            post_gather[0] = True
            if os.environ.get("K_DBG1"):
                continue
            kb.op("pool", lambda e: e.collective_compute(
                "AllGather", ALU.bypass, replica_groups=[list(range(NCORES))],
                ins=[kvl.ap().opt()], outs=[kva.ap().opt()]), writes=[r_dram, r_kva], chan=kb.chan("cc"), cinc=1)
            continue
        wb, r_wb, c_wb = wring.next()
        if not post_gather[0]:
            kb.op("pool", lambda e, wb=wb, c0=c0, ncols=ncols: e.dma_start(
                out=wb[:, :, :ncols], in_=w_in[:, c0:c0 + ncols].rearrange("(k p) n -> p k n", p=128)),
                writes=[r_wb], chan=c_wb)
        else:
            for hf in range(2):
                stg, r_stg, c_stg = stgr.next()
                cs = slice(hf * 256, (hf + 1) * 256)
                kb.op("sp", lambda e, stg=stg, c0=c0, hf=hf: e.dma_start(
                    out=stg, in_=w_in[:, c0 + hf * 256:c0 + (hf + 1) * 256].rearrange("(k p) n -> p k n", p=128)),
                    writes=[r_stg], chan=c_stg)
                kb.op("pool", lambda e, stg=stg, wb=wb, cs=cs: e.tensor_copy(out=wb[:, :, cs], in_=stg),
                      reads=[r_stg], writes=[r_wb])
        for t in range(NT):
            nv = tile_rows(t)
            pi = mm_banks[cnt["mm"] % 3]
            cnt["mm"] += 1
            for k in range(16):
                kb.op("pe", lambda e, k=k, t=t, nv=nv, pi=pi, wb=wb, ncols=ncols: e.matmul(
                    ps_f32(pi)[:nv, :ncols], lhsT=hT[:, k, t * 128:t * 128 + nv], rhs=wb[:, k, :ncols],
                    start=(k == 0), stop=(k == 15)), reads=[r_hT[t], r_wb], writes=[psr[pi]])
            evac(kind, b, t, pi, ncols)
    kb.barrier()
    ar.release(m0)
    m2 = ar.mark()
    oTA = nc.dram_tensor("oTA", [8, 128, TOK], BF16).ap()
    oTB = nc.dram_tensor("oTB", [8, 128, TOK], BF16).ap()
    nmask = nc.dram_tensor("nmask", [NT, 128, 16384], BF16).ap()
    x1s = nc.dram_tensor("x1s", [TOK, D], F32).ap()
    kva_ap = kva.ap()
    kva_x = kva_ap.rearrange("(r x) c -> x r c", x=NROWS)

    selc = ar.get([27], F32)
    r_sel = kb.R("sel")
    kb.op("sp", lambda e: e.dma_start(out=selc, in_=c_sel.partition_broadcast(128)), writes=[r_sel],
          chan=kb.chan("c1"))
    b15 = ar.get([16], F32)
    r_b15 = kb.R("b15")
    kb.op("sp", lambda e: e.dma_start(out=b15, in_=rel_bias[15, :].partition_broadcast(128)), writes=[r_b15],
          chan=kb.chan("c1"))
    cm = ar.get([128], F32)
    r_cm = kb.R("cm")
    kb.op("sp", lambda e: e.dma_start(out=cm, in_=c_cm), writes=[r_cm], chan=kb.chan("c1"))
    lamt = ar.get([4, 64], F32)
    r_lam = kb.R("lam")
    kb.op("sp", lambda e: e.dma_start(out=lamt, in_=lamv.rearrange("a b -> (a b)").partition_broadcast(128)),
          writes=[r_lam], chan=kb.chan("c1"))
    lsc = ar.get([16], F32)
    junk64 = ar.get([64], F32)
    r_j64 = kb.R("j64")
    for i in range(2):
        kb.op("dve", lambda e, i=i: e.tensor_tensor(out=junk64, in0=lamt[:, 2 * i, :], in1=lamt[:, 2 * i + 1, :],
                                                    op=ALU.mult), reads=[r_lam], writes=[r_j64])
        kb.op("dve", lambda e, i=i: e.tensor_reduce(out=lsc[:, i:i + 1], in_=junk64, axis=AX.X, op=ALU.add),
              reads=[r_j64], writes=[r_lam])
        kb.op("act", lambda e, i=i: e.activation(out=lsc[:, 2 + i:3 + i], in_=lsc[:, i:i + 1], func=AF.Exp),
              reads=[r_lam], writes=[r_lam])
    LAM_INIT = 0.8 - 0.6 * 1.0
    kb.op("dve", lambda e: e.tensor_tensor(out=lsc[:, 4:5], in0=lsc[:, 2:3], in1=lsc[:, 3:4], op=ALU.subtract),
          reads=[r_lam], writes=[r_lam])
    kb.op("dve", lambda e: e.tensor_scalar(out=lsc[:, 5:6], in0=lsc[:, 4:5], scalar1=LAM_INIT, scalar2=-1.0,
                                           op0=ALU.add, op1=ALU.mult), reads=[r_lam], writes=[r_lam])
    idxadd = ar.get([8, 128], F32)
    r_idxadd = kb.R("idxadd")
    for r in range(8):
        kb.op("dve", lambda e, r=r: e.tensor_scalar(out=idxadd[:, r, :], in0=cm, scalar1=selc[:, 9 + r:10 + r],
                                                    scalar2=BIGNEG / NEG, op0=ALU.mult, op1=ALU.mult),
              reads=[r_cm, r_sel], writes=[r_idxadd])
        kb.op("dve", lambda e, r=r: e.tensor_scalar(out=junk64[:, 0:1], in0=selc[:, 18 + r:19 + r], scalar1=BIGNEG,
                                                    scalar2=None, op0=ALU.mult), reads=[r_sel], writes=[r_j64])
        kb.op("dve", lambda e, r=r: e.tensor_scalar(out=idxadd[:, r, :], in0=idxadd[:, r, :],
                                                    scalar1=junk64[:, 0:1], scalar2=None, op0=ALU.add),
              reads=[r_j64, r_idxadd], writes=[r_idxadd])
    bnr = ar.get([16, 2, 128], F32)
    r_bnr = kb.R("bnr")
    kb.op("sp", lambda e: e.dma_start(out=bnr, in_=c_bn.rearrange("h t q s -> q h t s")), writes=[r_bnr],
          chan=kb.chan("c1"))
    for h in range(16):
        kb.op("dve", lambda e, h=h: e.tensor_scalar(out=bnr[:, h, :, :], in0=bnr[:, h, :, :],
                                                    scalar1=b15[:, h:h + 1], scalar2=None, op0=ALU.subtract),
              reads=[r_bnr, r_b15], writes=[r_bnr])
        kb.op("dve", lambda e, h=h: e.tensor_tensor(out=bnr[:, h, 0, :], in0=bnr[:, h, 0, :], in1=cm, op=ALU.add),
              reads=[r_bnr, r_cm], writes=[r_bnr])
    m2b = ar.mark()

    def tile_ctx(t):
        out = []
        if t < NPT:
            j = t
            for r in range(8):
                for jb in range(j + 1):
                    kind = "far"
                    if jb == j:
                        kind = r
                    elif jb == j - 1 and r == 7:
                        kind = 8
                    out.append((kind, r, jb, 128))
        else:
            for jb in range(9):
                kind = "far"
                if jb == 7:
                    kind = "sprev"
                if jb == 8:
                    kind = "snew"
                out.append((kind, 0, jb, 128 if jb < 8 else DEC))
        return out

    kit = ar.get([8, 2048], BF16)
    r_kit = kb.R("kit")
    kb.op("sp", lambda e: e.dma_start(out=kit, in_=kva_x[R_KIT:R_KIT + 128, :, :]), reads=[r_kva], writes=[r_kit],
          chan=kb.chan("kit"))
    kits = ar.get([SK], BF16)
    r_kits = kb.R("kits")
    sc = ar.get([16384], F32)
    r_sc = kb.R("sc")
    jbig = ar.get([16384], BF16)
    r_jbig = kb.R("jbig")
    qit = ar.get([8, 128], BF16)
    r_qit = kb.R("qit")
    tmpr = Ring("itmp", 3, [512], F32)
    bs = ar.get([16], F32)
    r_bs = kb.R("bs")
    NITER = 22
    idx_banks = [0, 1, 2, 3]
    for t in range(NT):
        nv = tile_rows(t)
        cols = slice(t * 128, t * 128 + nv)
        kb.op("sp", lambda e, cols=cols, nv=nv: e.dma_start(out=qit[:, :, :nv],
                                                            in_=qTI[:, :, cols].rearrange("h d t -> d h t")),
              writes=[r_qit], chan=kb.chan("qit"))
        if t >= NPT:
            s = t - NPT
            kb.op("sp", lambda e, s=s: e.dma_start(out=kits, in_=kvs[s, R_KIT:R_KIT + 128, :]), writes=[r_kits],
                  chan=kb.chan("kit"))
            segs = [(kits, r_kits, 0, 512, 0), (kits, r_kits, 512, 512, 512), (kits, r_kits, 1024, DEC, 1024)]
            ntot = PAST + DEC
        else:
            j = t
            L = (j + 1) * 128
            segs = []
            for r in range(8):
                for c0 in range(0, L, 512):
                    n = min(512, L - c0)
                    segs.append((kit[:, r, :], r_kit, c0, n, r * L + c0))
            ntot = 8 * L
        for (ksrc, r_ks, c0, n, o0) in segs:
            for hd in range(16):
                pr, hf = hd // 2, hd % 2
                pi = idx_banks[cnt["mm"] % 4]
                cnt["mm"] += 1
                kb.op("pe", lambda e, pr=pr, hf=hf, pi=pi, ksrc=ksrc, c0=c0, n=n, nv=nv: e.matmul(
                    ps_f32(pi)[:nv, :n], lhsT=qit[hf * 64:(hf + 1) * 64, pr, :nv],
                    rhs=ksrc[hf * 64:(hf + 1) * 64, c0:c0 + n], start=True, stop=True),
                    reads=[r_qit, r_ks], writes=[psr[pi]])
                if hd == 0:
                    kb.op("dve", lambda e, pi=pi, n=n, o0=o0, nv=nv, t=t: e.tensor_scalar(
                        out=sc[:nv, o0:o0 + n], in0=ps_f32(pi)[:nv, :n], scalar1=0.0, scalar2=wiT[:nv, t, 0:1],
                        op0=ALU.max, op1=ALU.mult), reads=[psr[pi], r_wi[t]], writes=[r_sc])
                else:
                    tm, r_tm, _ = tmpr.next()
                    kb.op("dve", lambda e, pi=pi, n=n, nv=nv, t=t, hd=hd, tm=tm: e.tensor_scalar(
                        out=tm[:nv, :n], in0=ps_f32(pi)[:nv, :n], scalar1=0.0, scalar2=wiT[:nv, t, hd:hd + 1],
                        op0=ALU.max, op1=ALU.mult), reads=[psr[pi], r_wi[t]], writes=[r_tm])
                    kb.op("pool", lambda e, n=n, o0=o0, nv=nv, tm=tm: e.tensor_tensor(
                        out=sc[:nv, o0:o0 + n], in0=sc[:nv, o0:o0 + n], in1=tm[:nv, :n], op=ALU.add),
                        reads=[r_tm, r_sc], writes=[r_sc])
        kb.op("dve", lambda e, nv=nv, ntot=ntot: e.tensor_reduce(out=bs[:nv, 5:6], in_=sc[:nv, :ntot], axis=AX.X,
                                                                 op=ALU.max), reads=[r_sc], writes=[r_bs])
        kb.op("dve", lambda e, nv=nv, ntot=ntot: e.tensor_reduce(out=bs[:nv, 0:1], in_=sc[:nv, :ntot], axis=AX.X,
                                                                 op=ALU.min), reads=[r_sc], writes=[r_bs])
        kb.op("dve", lambda e, nv=nv: e.tensor_tensor(out=bs[:nv, 1:2], in0=bs[:nv, 5:6], in1=bs[:nv, 0:1],
                                                      op=ALU.subtract), reads=[r_bs], writes=[r_bs])
        kb.op("dve", lambda e, nv=nv: e.tensor_scalar(out=bs[:nv, 1:2], in0=bs[:nv, 1:2], scalar1=0.5,
                                                      scalar2=1e-6, op0=ALU.mult, op1=ALU.add),
              reads=[r_bs], writes=[r_bs])
        if t < NPT:
            L = (t + 1) * 128
            scv = sc[:, 0:8 * L].rearrange("p (r l) -> p r l", l=L)[:, :, t * 128:(t + 1) * 128]
            kb.op("pool", lambda e, scv=scv: e.tensor_tensor(out=scv, in0=scv, in1=idxadd, op=ALU.add),
                  reads=[r_sc, r_idxadd], writes=[r_sc])
        for it in range(NITER):
            kb.op("dve", lambda e, nv=nv: e.tensor_tensor(out=bs[:nv, 2:3], in0=bs[:nv, 0:1], in1=bs[:nv, 1:2],
                                                          op=ALU.add), reads=[r_bs], writes=[r_bs])
            kb.op("dve", lambda e, nv=nv, ntot=ntot: e.tensor_scalar(
                out=jbig[:nv, :ntot], in0=sc[:nv, :ntot], scalar1=bs[:nv, 2:3], scalar2=None, op0=ALU.is_ge,
                op1=ALU.add, accum_out=bs[:nv, 3:4]), reads=[r_sc, r_bs], writes=[r_jbig, r_bs])
            kb.op("dve", lambda e, nv=nv: e.tensor_scalar(out=bs[:nv, 4:5], in0=bs[:nv, 3:4], scalar1=float(TOPK),
                                                          scalar2=bs[:nv, 1:2], op0=ALU.is_ge, op1=ALU.mult),
                  reads=[r_bs], writes=[r_bs])
            kb.op("dve", lambda e, nv=nv: e.tensor_tensor(out=bs[:nv, 0:1], in0=bs[:nv, 0:1], in1=bs[:nv, 4:5],
                                                          op=ALU.add), reads=[r_bs], writes=[r_bs])
            kb.op("dve", lambda e, nv=nv: e.tensor_scalar(out=bs[:nv, 1:2], in0=bs[:nv, 1:2], scalar1=0.5,
                                                          scalar2=None, op0=ALU.mult), reads=[r_bs], writes=[r_bs])
        kb.op("dve", lambda e, nv=nv, ntot=ntot: e.tensor_scalar(
            out=jbig[:nv, :ntot], in0=sc[:nv, :ntot], scalar1=bs[:nv, 0:1], scalar2=NEG, op0=ALU.is_lt,
            op1=ALU.mult), reads=[r_sc, r_bs], writes=[r_jbig])
        kb.op("sp", lambda e, nv=nv, ntot=ntot, t=t: e.dma_start(out=nmask[t, :nv, :ntot], in_=jbig[:nv, :ntot]),
              reads=[r_jbig, r_dram], chan=kb.chan("nm"))
    kb.barrier()
    ar.release(m2b)
    import os
    if os.environ.get("KDBG") == "2a":
        dbgb = ar.get([2048], BF16)
        dbg = ar.get([2048], F32)
        r_dbg = kb.R("dbg")
        kb.op("sp", lambda e: e.dma_start(out=dbgb, in_=nmask[2, :, 0:2048]), writes=[r_dbg], chan=kb.chan("dbg"))
        kb.op("dve", lambda e: e.tensor_copy(out=dbg, in_=dbgb), reads=[r_dbg], writes=[r_dbg])
        kb.op("sp", lambda e: e.dma_start(out=yp[0:128, :], in_=dbg), reads=[r_dbg], chan=kb.chan("dbg"))
        kb.op("sp", lambda e: e.dma_start(out=dbgb, in_=nmask[2, :, 2048:4096]), reads=[r_dbg], writes=[r_dbg], chan=kb.chan("dbg"))
        kb.op("dve", lambda e: e.tensor_copy(out=dbg, in_=dbgb), reads=[r_dbg], writes=[r_dbg])
        kb.op("sp", lambda e: e.dma_start(out=yp[128:256, :], in_=dbg), reads=[r_dbg], chan=kb.chan("dbg"))
        kb.emit()
        es.close()
        return nc
    kth = ar.get([8, 2048], BF16)
    r_kth = kb.R("kth")
    vth = ar.get([128, 130], BF16)
    r_vth = kb.R("vth")
    kts = [ar.get([SK], BF16) for _ in range(NST)]
    vts = [ar.get([9, 130], BF16) for _ in range(NST)]
    qth = ar.get([TOK], BF16)
    r_qth = kb.R("qth")
    qz = ar.get([2, TOK], BF16)
    nmt = ar.get([16384], BF16)
    r_nmt = kb.R("nmt")
    addt = ar.get([9, 128], BF16)
    r_addt = kb.R("addt")
    addf = ar.get([128], F32)
    r_addf = kb.R("addf")
    bnb = ar.get([2, 128], BF16)
    r_bnb = kb.R("bnb")
    negs = ar.get([9], F32)
    r_negs = kb.R("negs")
    ptr = Ring("pT", 4, [256], BF16)
    EXP_DIRECT = not os.environ.get("K_NOEXPD")
    pfr = Ring("pF", 3, [256], F32)
    osr = Ring("os", 2, [128], F32)
    o2r = Ring("o2", 2, [128], F32)
    fs = ar.get([16], F32)
    r_fs = kb.R("fs")
    kb.op("dve", lambda e: e.memset(vth[:, :, 128:129], 1.0), writes=[r_vth])
    kb.op("dve", lambda e: e.memset(qz, 0.0), writes=[r_qth])
    for s in range(NST):
        kb.op("dve", lambda e, s=s: e.memset(vts[s][:, :, 128:129], 1.0), writes=[r_vth])
    kb.op("dve", lambda e: e.tensor_scalar(out=negs, in0=selc[:, 18:27], scalar1=NEG, scalar2=None, op0=ALU.mult),
          reads=[r_sel], writes=[r_negs])
    kb.op("dve", lambda e: e.tensor_scalar(out=gains["sub"][0], in0=gains["sub"][0], scalar1=1.0 - LAM_INIT,
                                           scalar2=None, op0=ALU.mult), reads=[gains["sub"][1]],
          writes=[gains["sub"][1]])
    s_banks = [0, 1, 2, 3]

    def load_head(mixer, hh):
        rk = (R_KTA if mixer == "A" else R_KTB) + hh * 128
        rv = (R_VA if mixer == "A" else R_VB) + hh * 128
        kb.op("sp", lambda e: e.dma_start(out=kth, in_=kva_x[rk:rk + 128, :, :]), reads=[r_kva], writes=[r_kth],
              chan=kb.chan("kth"))
        for r in range(8):
            kb.op("sp", lambda e, r=r: e.dma_start(out=vth[:, r * 16:(r + 1) * 16, 0:128],
                                                   in_=kva_x[rv:rv + 128, r, :].rearrange("p (j d) -> p j d", d=128)),
                  reads=[r_kva], writes=[r_vth], chan=kb.chan("vth"))
        for s in range(NST):
            kb.op("sp", lambda e, s=s: e.dma_start(out=kts[s], in_=kvs[s, rk:rk + 128, :]), writes=[r_kth],
                  chan=kb.chan("kth"))
            kb.op("sp", lambda e, s=s: e.dma_start(out=vts[s][:, :, 0:128],
                                                   in_=kvs[s, rv:rv + 128, :].rearrange("p (j d) -> p j d", d=128)),
                  writes=[r_vth], chan=kb.chan("vth"))

    def prep_head(h16, qsrc, hq):
        if h16 < 8:
            kb.op("sp", lambda e: e.dma_start(out=qz[0:64, 0, :], in_=qsrc[hq, 0:64, :]), writes=[r_qth],
                  chan=kb.chan("qth"))
            kb.op("sp", lambda e: e.dma_start(out=qz[64:128, 1, :], in_=qsrc[hq, 64:128, :]), writes=[r_qth],
                  chan=kb.chan("qth"))
        else:
            kb.op("sp", lambda e: e.dma_start(out=qth, in_=qsrc[hq]), writes=[r_qth], chan=kb.chan("qth"))
        for slot in range(9):
            kb.op("pool", lambda e, slot=slot: e.tensor_scalar(
                out=addf, in0=bnr[:, h16, 1, :], scalar1=selc[:, slot:slot + 1], scalar2=negs[:, slot:slot + 1],
                op0=ALU.mult, op1=ALU.add), reads=[r_bnr, r_sel, r_negs], writes=[r_addf])
            kb.op("dve", lambda e, slot=slot: e.scalar_tensor_tensor(
                out=addt[:, slot, :], in0=bnr[:, h16, 0, :], scalar=selc[:, 9 + slot:10 + slot], in1=addf,
                op0=ALU.mult, op1=ALU.add), reads=[r_bnr, r_sel, r_addf], writes=[r_addt])
        kb.op("pool", lambda e: e.tensor_copy(out=bnb, in_=bnr[:, h16, :, :]), reads=[r_bnr], writes=[r_bnb])

    def attend(mixer, h16, hq, t):
        nv = tile_rows(t)
        cols = slice(t * 128, t * 128 + nv)
        ctx = tile_ctx(t)
        ncomp = 2 if mixer == "A" else 1
        kd = 64 if mixer == "A" else 128
        L = (t + 1) * 128
        if mixer == "B":
            ntot = 8 * L if t < NPT else PAST + DEC
            kb.op("sp", lambda e: e.dma_start(out=nmt[:nv, :ntot], in_=nmask[t, :nv, :ntot]), writes=[r_nmt],
                  chan=kb.chan("nmt"))
        obank = [4, 5]
        DPIPE = 2
        nblk = len(ctx)
        pend = {}

        def stage_qk(i):
            kind, r, jb, n = ctx[i]
            if t < NPT:
                kblk = kth[:, r, jb * 128:jb * 128 + n]
                vblk = vth[:n, r * 16 + jb, 0:129]
                o0 = r * L + jb * 128
            else:
                s = t - NPT
                kblk = kts[s][:, jb * 128:jb * 128 + n]
                vblk = vts[s][:n, jb, 0:129]
                o0 = jb * 128
            pi = s_banks[i % 4]
            special = kind != "far"
            addap, r_add = None, None
            if kind == "sprev":
                addap, r_add = bnb[:nv, 1, :n], r_bnb
            elif kind == "snew":
                addap, r_add = bnb[:nv, 0, :n], r_bnb
            elif special:
                addap, r_add = addt[:nv, kind, :n], r_addt
            for c in range(ncomp):
                last = not special and mixer == "A"
                kb.op("pe", lambda e, c=c: e.matmul(
                    ps_f32(pi)[:n, c * nv:(c + 1) * nv], lhsT=kblk[:, :n],
                    rhs=(qz[:, c, cols] if mixer == "A" else qth[:, cols]), start=True, stop=last),
                    reads=[r_kth, r_qth], writes=[psr[pi]])
                if mixer == "B":
                    kb.op("pe", lambda e, c=c: e.matmul(
                        ps_f32(pi)[:n, c * nv:(c + 1) * nv], lhsT=nmt[:nv, o0:o0 + n], rhs=ident[:nv, :nv],
                        start=False, stop=not special), reads=[r_nmt, r_ident], writes=[psr[pi]])
                if special:
                    kb.op("pe", lambda e, c=c: e.matmul(
                        ps_f32(pi)[:n, c * nv:(c + 1) * nv], lhsT=addap, rhs=ident[:nv, :nv],
                        start=False, stop=True), reads=[r_add, r_ident], writes=[psr[pi]])
            pt, r_pt, _ = ptr.next()
            if EXP_DIRECT:
                kb.op("act", lambda e: e.activation(
                    out=pt[:n, 0:ncomp * nv], in_=ps_f32(pi)[:n, 0:ncomp * nv], func=AF.Exp),
                    reads=[psr[pi]], writes=[r_pt])
            else:
                pf, r_pf, _ = pfr.next()
                kb.op("act", lambda e: e.activation(
                    out=pf[:n, 0:ncomp * nv], in_=ps_f32(pi)[:n, 0:ncomp * nv], func=AF.Exp),
                    reads=[psr[pi]], writes=[r_pf])
                kb.op("pool", lambda e: e.tensor_copy(out=pt[:n, 0:ncomp * nv], in_=pf[:n, 0:ncomp * nv]),
                      reads=[r_pf], writes=[r_pt])
            pend[i] = (n, pt, r_pt, vblk)

        def stage_pv(i):
            n, pt, r_pt, vblk = pend.pop(i)
            for c in range(ncomp):
                kb.op("pe", lambda e, c=c: e.matmul(
                    ps_f32(obank[c])[:nv, 0:129], lhsT=pt[:n, c * nv:(c + 1) * nv], rhs=vblk,
                    start=(i == 0), stop=(i == nblk - 1)), reads=[r_pt, r_vth], writes=[psr[obank[c]]])

        for i in range(nblk + DPIPE):
            if i < nblk:
                stage_qk(i)
            if i >= DPIPE:
                stage_pv(i - DPIPE)
        osb, r_os, _ = osr.next()
        kb.op("dve", lambda e: e.reciprocal(out=fs[:nv, 0:1], in_=ps_f32(4)[:nv, 128:129]), reads=[psr[4]],
              writes=[r_fs])
        kb.op("dve", lambda e: e.tensor_scalar(out=osb[:nv, :], in0=ps_f32(4)[:nv, 0:128], scalar1=fs[:nv, 0:1],
                                               scalar2=None, op0=ALU.mult), reads=[psr[4], r_fs], writes=[r_os])
        if mixer == "A":
            kb.op("dve", lambda e: e.reciprocal(out=fs[:nv, 1:2], in_=ps_f32(5)[:nv, 128:129]), reads=[psr[5]],
                  writes=[r_fs])
            kb.op("dve", lambda e: e.tensor_tensor(out=fs[:nv, 2:3], in0=fs[:nv, 1:2], in1=lsc[:nv, 5:6],
                                                   op=ALU.mult), reads=[r_fs, r_lam], writes=[r_fs])
            o2, r_o2, _ = o2r.next()
            kb.op("dve", lambda e: e.scalar_tensor_tensor(out=o2[:nv, :], in0=ps_f32(5)[:nv, 0:128],
                                                          scalar=fs[:nv, 2:3], in1=osb[:nv, :], op0=ALU.mult,
                                                          op1=ALU.add), reads=[psr[5], r_fs, r_os], writes=[r_o2])
            kb.op("act", lambda e: e.activation(out=osb[:nv, :], in_=o2[:nv, :], func=AF.Square,
                                                accum_out=fs[:nv, 3:4]), reads=[r_o2], writes=[r_os, r_fs])
            kb.op("act", lambda e: e.activation(out=fs[:nv, 4:5], in_=fs[:nv, 3:4], func=AF.Ln, bias=EPS,
                                                scale=1.0 / 128), reads=[r_fs], writes=[r_fs])
            kb.op("act", lambda e: e.activation(out=fs[:nv, 5:6], in_=fs[:nv, 4:5], func=AF.Exp, scale=-0.5),
                  reads=[r_fs], writes=[r_fs])
            kb.op("dve", lambda e: e.scalar_tensor_tensor(out=osb[:nv, :], in0=o2[:nv, :], scalar=fs[:nv, 5:6],
                                                          in1=gains["sub"][0][:nv, :], op0=ALU.mult, op1=ALU.mult),
                  reads=[r_o2, r_fs, gains["sub"][1]], writes=[r_os])
        zb, r_zb, _ = zbr.next()
        kb.op("act", lambda e: e.copy(out=zb[:nv, 0:128], in_=osb[:nv, :]), reads=[r_os], writes=[r_zb])
        od = oTA if mixer == "A" else oTB
        transpose_store(zb, r_zb, nv, 1, od[hq:hq + 1, :, cols].rearrange("h d t -> d h t"))

    import os
    NHA = int(os.environ.get("K_NHA", "8"))
    NHB = int(os.environ.get("K_NHB", "8"))
    TL = [int(x) for x in os.environ.get("K_TILES", ",".join(str(i) for i in range(NT))).split(",")]
    for h in range(NHA):
        if not os.environ.get("K_NOLOAD"):
            load_head("A", h)
        if not os.environ.get("K_NOPREP"):
            prep_head(h, qTA, h)
        for t in ([] if os.environ.get("K_NOATT") else TL):
            attend("A", h, h, t)
    for hbi in range(NHB):
        if hbi % 4 == 0:
            load_head("B", hbi // 4)
        prep_head(8 + hbi, qTB, hbi)
        for t in TL:
            attend("B", 8 + hbi, hbi, t)
    if os.environ.get("KDBG") == "2b":
        kb.emit()
        es.close()
        return nc
    kb.barrier()
    ar.release(m2b)
    ar.release(m2)
    m3 = ar.mark()
    woa = ar.get([8, 2048], BF16)
    wob = ar.get([8, 2048], BF16)
    wout = ar.get([16, 2048], BF16)
    r_w3 = kb.R("w3")
    stg3 = Ring("stg3", 2, [8, 256], F32)
    for (wsrc, wdst, nk) in ((w_o_a, woa, 8), (w_o_b, wob, 8), (w_out, wout, 16)):
        ncol = 256 if nk == 8 else 128
        for c0 in range(0, 2048, ncol):
            stg, r_stg, c_stg = stg3.next()
            sv = stg.rearrange("p a b -> p (a b)")[:, 0:nk * ncol].rearrange("p (a b) -> p a b", b=ncol)
            kb.op("sp", lambda e, sv=sv, wsrc=wsrc, c0=c0, ncol=ncol: e.dma_start(
                out=sv, in_=wsrc[:, c0:c0 + ncol].rearrange("(k p) n -> p k n", p=128)), writes=[r_stg], chan=c_stg)
            kb.op("pool", lambda e, sv=sv, wdst=wdst, c0=c0, ncol=ncol: e.tensor_copy(
                out=wdst[:, :, c0:c0 + ncol], in_=sv), reads=[r_stg], writes=[r_w3])
    oat = Ring("oat", 1, [8, 128], BF16)
    obt = Ring("obt", 1, [8, 128], BF16)
    gtr = Ring("gt", 2, [2, 512], F32)
    xtr = Ring("xt3", 1, [D], F32)
    mgr = Ring("mg", 2, [512], F32)
    mg2r = Ring("mg2", 2, [512], F32)
    mgbr = Ring("mgb", 1, [D], BF16)
    mTr = Ring("mT", 1, [16, 128], BF16)
    for t in range(NT):
        nv = tile_rows(t)
        cols = slice(t * 128, t * 128 + nv)
        oa_t, r_oa, c_oa = oat.next()
        ob_t, r_ob, c_ob = obt.next()
        xt3, r_xt3, c_xt3 = xtr.next()
        kb.op("sp", lambda e, oa_t=oa_t, cols=cols, nv=nv: e.dma_start(
            out=oa_t[:, :, :nv], in_=oTA[:, :, cols].rearrange("h d t -> d h t")), writes=[r_oa], chan=c_oa)
        kb.op("sp", lambda e, ob_t=ob_t, cols=cols, nv=nv: e.dma_start(
            out=ob_t[:, :, :nv], in_=oTB[:, :, cols].rearrange("h d t -> d h t")), writes=[r_ob], chan=c_ob)
        kb.op("sp", lambda e, xt3=xt3, t=t, nv=nv: e.dma_start(out=xt3[:nv, :], in_=x_rows(t)), writes=[r_xt3],
              chan=c_xt3)
        mgb, r_mgb, _ = mgbr.next()
        for cb in range(4):
            cs = slice(cb * 512, (cb + 1) * 512)
            pa, pb = cb % 2, 2 + cb % 2
            gt, r_gt, c_gt = gtr.next()
            for gi in range(2):
                kb.op("sp", lambda e, gt=gt, t=t, nv=nv, gi=gi, cb=cb: e.dma_start(
                    out=gt[:nv, gi, :], in_=gates[t * 128:t * 128 + nv, gi * 2048 + cb * 512:gi * 2048 + (cb + 1) * 512]),
                    writes=[r_gt], chan=c_gt)
            for k in range(8):
                kb.op("pe", lambda e, k=k, pa=pa, cs=cs, oa_t=oa_t, nv=nv: e.matmul(
                    ps_f32(pa)[:nv, :], lhsT=oa_t[:, k, :nv], rhs=woa[:, k, cs], start=(k == 0), stop=(k == 7)),
                    reads=[r_oa, r_w3], writes=[psr[pa]])
            for k in range(8):
                kb.op("pe", lambda e, k=k, pb=pb, cs=cs, ob_t=ob_t, nv=nv: e.matmul(
                    ps_f32(pb)[:nv, :], lhsT=ob_t[:, k, :nv], rhs=wob[:, k, cs], start=(k == 0), stop=(k == 7)),
                    reads=[r_ob, r_w3], writes=[psr[pb]])
            mg, r_mg, _ = mgr.next()
            mg2, r_mg2, _ = mg2r.next()
            kb.op("dve", lambda e, pa=pa, cs=cs, mg=mg, gt=gt, nv=nv: e.tensor_tensor(
                out=mg[:nv, :], in0=ps_f32(pa)[:nv, :], in1=gt[:nv, 0, :], op=ALU.mult),
                reads=[psr[pa], r_gt], writes=[r_mg])
            kb.op("dve", lambda e, pb=pb, cb=cb, mg2=mg2, gt=gt, nv=nv: e.tensor_tensor(
                out=mg2[:nv, :], in0=ps_f32(pb)[:nv, :], in1=gt[:nv, 1, :],
                op=ALU.mult), reads=[psr[pb], r_gt], writes=[r_mg2])
            kb.op("pool", lambda e, mg=mg, mg2=mg2, nv=nv: e.tensor_tensor(out=mg[:nv, :], in0=mg[:nv, :],
                                                                           in1=mg2[:nv, :], op=ALU.add),
                  reads=[r_mg, r_mg2], writes=[r_mg])
            kb.op("act", lambda e, mg=mg, mgb=mgb, cs=cs, nv=nv: e.copy(out=mgb[:nv, cs], in_=mg[:nv, :]),
                  reads=[r_mg], writes=[r_mgb])
        mT, r_mT, _ = mTr.next()
        for half in range(2):
            pi = 6 + half
            for kk in range(8):
                k = half * 8 + kk
                kb.op("pe", lambda e, k=k, kk=kk, pi=pi, mgb=mgb, nv=nv: e.transpose(
                    out=ps_bf16(pi)[:, kk * 128:kk * 128 + nv], in_=mgb[:nv, k * 128:(k + 1) * 128],
                    identity=ident[:nv, :nv]), reads=[r_mgb, r_ident], writes=[psr[pi]])
            kb.op("dve", lambda e, pi=pi, half=half, mT=mT, nv=nv: e.tensor_copy(
                out=mT[:, half * 8:(half + 1) * 8, :nv],
                in_=ps_bf16(pi).rearrange("p (a b) -> p a b", b=128)[:, :, :nv]), reads=[psr[pi]], writes=[r_mT])
        x1, r_x1, c_x1 = xt3, r_xt3, c_xt3
        for cb in range(4):
            cs = slice(cb * 512, (cb + 1) * 512)
            pi = 4 + cb % 2
            for k in range(16):
                kb.op("pe", lambda e, k=k, pi=pi, cs=cs, mT=mT, nv=nv: e.matmul(
                    ps_f32(pi)[:nv, :], lhsT=mT[:, k, :nv], rhs=wout[:, k, cs], start=(k == 0), stop=(k == 15)),
                    reads=[r_mT, r_w3], writes=[psr[pi]])
            kb.op("dve", lambda e, pi=pi, cs=cs, x1=x1, xt3=xt3, nv=nv: e.tensor_tensor(
                out=x1[:nv, cs], in0=ps_f32(pi)[:nv, :], in1=xt3[:nv, cs], op=ALU.add),
                reads=[psr[pi], r_xt3], writes=[r_xt3])
        kb.op("sp", lambda e, x1=x1, t=t, nv=nv: e.dma_start(out=x1s[t * 128:t * 128 + nv, :], in_=x1[:nv, :]),
              reads=[r_x1, r_dram], chan=c_x1)
    kb.barrier()
    ar.release(m3)
    GT = 6
    g2bc = ar.get([D], F32)
    r_g2 = kb.R("g2")
    kb.op("sp", lambda e: e.dma_start(out=g2bc, in_=norm2_g.partition_broadcast(128)), writes=[r_g2],
          chan=kb.chan("c1"))
    yacc = [ar.get([D], F32) for _ in range(GT)]
    r_y = [kb.R("y%d" % i) for i in range(GT)]
    h2T = ar.get([16, GT * 128], BF16)
    r_h2T = kb.R("h2T")
    hb3 = ar.get([D], BF16)
    r_hb3 = kb.R("hb3")
    junk3 = ar.get([D], BF16)
    r_junk3 = kb.R("junk3")
    ss3 = ar.get([8], F32)
    r_ss3 = kb.R("ss3")
    w1r = Ring("w1s", 2, [16, 256], BF16)
    w2r = Ring("w2s", 2, [2, 2048], BF16)
    s1r = Ring("s1", 1, [16, 256], F32)
    s2r = Ring("s2", 1, [2, 2048], F32)
    uTr = Ring("uT", 2, [2, GT * 128], BF16)
    rlr = Ring("rl", 2, [512], F32)
    NSUB = int(os.environ.get("K_NSUB", "32"))
    for g0 in range(0, NT, GT):
        tiles = list(range(g0, g0 + GT))
        for i, t in enumerate(tiles):
            nv = tile_rows(t)
            kb.op("sp", lambda e, i=i, t=t, nv=nv: e.dma_start(out=yacc[i][:nv, :], in_=x1s[t * 128:t * 128 + nv, :]),
                  writes=[r_y[i]], chan=kb.chan("yl%d" % i))
            kb.op("act", lambda e, i=i, nv=nv: e.activation(out=junk3[:nv, :], in_=yacc[i][:nv, :], func=AF.Square,
                                                          accum_out=ss3[:nv, 0:1]), reads=[r_y[i]],
                  writes=[r_junk3, r_ss3])
            kb.op("act", lambda e, nv=nv: e.activation(out=ss3[:nv, 1:2], in_=ss3[:nv, 0:1], func=AF.Ln, bias=EPS,
                                                     scale=1.0 / D), reads=[r_ss3], writes=[r_ss3])
            kb.op("act", lambda e, nv=nv: e.activation(out=ss3[:nv, 2:3], in_=ss3[:nv, 1:2], func=AF.Exp,
                                                     scale=-0.5), reads=[r_ss3], writes=[r_ss3])
            kb.op("dve", lambda e, i=i, nv=nv: e.scalar_tensor_tensor(
                out=hb3[:nv, :], in0=yacc[i][:nv, :], scalar=ss3[:nv, 2:3], in1=g2bc[:nv, :], op0=ALU.mult,
                op1=ALU.mult), reads=[r_y[i], r_ss3, r_g2], writes=[r_hb3])
            for half in range(2):
                pi = 6 + half
                for kk in range(8):
                    k = half * 8 + kk
                    kb.op("pe", lambda e, k=k, kk=kk, pi=pi, nv=nv: e.transpose(
                        out=ps_bf16(pi)[:, kk * 128:kk * 128 + nv], in_=hb3[:nv, k * 128:(k + 1) * 128],
                        identity=ident[:nv, :nv]), reads=[r_hb3, r_ident], writes=[psr[pi]])
                kb.op("dve", lambda e, pi=pi, half=half, i=i, nv=nv: e.tensor_copy(
                    out=h2T[:, half * 8:(half + 1) * 8, i * 128:i * 128 + nv],
                    in_=ps_bf16(pi).rearrange("p (a b) -> p a b", b=128)[:, :, :nv]), reads=[psr[pi]],
                    writes=[r_h2T])
            if nv < 128:
                kb.op("pool", lambda e, i=i, nv=nv: e.memset(h2T[:, :, i * 128 + nv:(i + 1) * 128], 0.0),
                      writes=[r_h2T])
        for sub in range(NSUB):
            f0 = sub * 256
            s1, r_s1, c_s1 = s1r.next()
            s2, r_s2, c_s2 = s2r.next()
            w1s, r_w1s, _ = w1r.next()
            w2s, r_w2s, _ = w2r.next()
            kb.op("sp", lambda e, s1=s1, f0=f0: e.dma_start(
                out=s1, in_=w_ff1[:, f0:f0 + 256].rearrange("(k p) n -> p k n", p=128)), writes=[r_s1], chan=c_s1)
            kb.op("sp", lambda e, s2=s2, f0=f0: e.dma_start(
                out=s2, in_=w_ff2[f0:f0 + 256, :].rearrange("(c p) n -> p c n", p=128)), writes=[r_s2], chan=c_s2)
            kb.op("act", lambda e, s1=s1, w1s=w1s: e.copy(out=w1s, in_=s1), reads=[r_s1], writes=[r_w1s])
            kb.op("pool", lambda e, s2=s2, w2s=w2s: e.tensor_copy(out=w2s, in_=s2), reads=[r_s2], writes=[r_w2s])
            uT, r_uT, _ = uTr.next()
            for fc in range(2):
                for (c0, n) in ((0, 512), (512, GT * 128 - 512)):
                    pi = cnt["mm"] % 2
                    cnt["mm"] += 1
                    for k in range(16):
                        kb.op("pe", lambda e, k=k, pi=pi, fc=fc, c0=c0, n=n, w1s=w1s: e.matmul(
                            ps_f32(pi)[:, :n], lhsT=w1s[:, k, fc * 128:(fc + 1) * 128], rhs=h2T[:, k, c0:c0 + n],
                            start=(k == 0), stop=(k == 15)), reads=[r_w1s, r_h2T], writes=[psr[pi]])
                    rl, r_rl, _ = rlr.next()
                    kb.op("act", lambda e, pi=pi, n=n, rl=rl: e.activation(out=rl[:, :n], in_=ps_f32(pi)[:, :n],
                                                                         func=AF.Relu), reads=[psr[pi]],
                          writes=[r_rl])
                    kb.op("act", lambda e, fc=fc, c0=c0, n=n, rl=rl, uT=uT: e.activation(
                        out=uT[:, fc, c0:c0 + n], in_=rl[:, :n], func=AF.Square), reads=[r_rl], writes=[r_uT])
            for i, t in enumerate(tiles):
                nv = tile_rows(t)
                for cb in range(4):
                    cs = slice(cb * 512, (cb + 1) * 512)
                    pi = 2 + cnt["tr"] % 4
                    cnt["tr"] += 1
                    for fc in range(2):
                        kb.op("pe", lambda e, fc=fc, pi=pi, i=i, nv=nv, cs=cs, uT=uT, w2s=w2s: e.matmul(
                            ps_f32(pi)[:nv, :], lhsT=uT[:, fc, i * 128:i * 128 + nv], rhs=w2s[:, fc, cs],
                            start=(fc == 0), stop=(fc == 1)), reads=[r_uT, r_w2s], writes=[psr[pi]])
                    kb.op("dve", lambda e, pi=pi, i=i, nv=nv, cs=cs: e.tensor_tensor(
                        out=yacc[i][:nv, cs], in0=ps_f32(pi)[:nv, :], in1=yacc[i][:nv, cs], op=ALU.add),
                        reads=[psr[pi], r_y[i]], writes=[r_y[i]])
        for i, t in enumerate(tiles):
            nv = tile_rows(t)
            dst = yp[t * 128:(t + 1) * 128, :] if t < NPT else ys[(t - NPT) * DEC:(t - NPT + 1) * DEC, :]
            kb.op("sp", lambda e, i=i, nv=nv, dst=dst: e.dma_start(out=dst, in_=yacc[i][:nv, :]), reads=[r_y[i]],
                  chan=kb.chan("yl%d" % i))
    kb.emit()
    es.close()
    return nc


def _t5_bucket(rel):
    import math
    half, exact = 16, 8
    n = np.abs(rel)
    nf = np.maximum(n, 1).astype(np.float32)
    large = exact + (np.log(nf / np.float32(exact)) / np.float32(math.log(128 / exact))
                     * np.float32(half - exact)).astype(np.int32)
    large = np.minimum(large, half - 1)
    return np.where(rel > 0, half, 0) + np.where(n < exact, n, large)


def _bias_layout(rel_bias):
    q = np.arange(128)[:, None]
    s_ = np.arange(128)[None, :]
    out = np.zeros((16, 2, 128, 128), np.float32)
    for ty, off in ((0, 0), (1, -128)):
        b = _t5_bucket(s_ - q + off)
        out[:, ty] = np.transpose(rel_bias[b], (2, 0, 1))
    return out


def _chunk_mask():
    m = np.zeros((128, 128), np.float32)
    m[0:64, 64:128] = NEG
    return m


def make_in_maps(inputs):
    f = lambda a: np.ascontiguousarray(np.asarray(a, dtype=np.float32))
    x_prompt = f(inputs["x_prompt"])[0]
    x_sample = f(inputs["x_sample"])
    xb = x_prompt.reshape(128, 128, D)
    shared = {
        "rel_bias": f(inputs["rel_bias"]),
        "norm1_g": f(inputs["norm1_g"])[0],
        "w_in": f(inputs["w_in"])[0],
        "qn_a_g": f(inputs["qn_a_g"])[0],
        "kn_a_g": f(inputs["kn_a_g"])[0],
        "lamv": np.stack([f(inputs[k])[0] for k in ("lam_q1", "lam_k1", "lam_q2", "lam_k2")], 0),
        "subln_a_g": f(inputs["subln_a_g"])[0],
        "qn_b_g": f(inputs["qn_b_g"])[0],
        "kn_b_g": f(inputs["kn_b_g"])[0],
        "w_o_a": f(inputs["w_o_a"])[0],
        "w_o_b": f(inputs["w_o_b"])[0],
        "w_out": f(inputs["w_out"])[0],
        "norm2_g": f(inputs["norm2_g"])[0],
        "w_ff1": f(inputs["w_ff1"])[0],
        "w_ff2": f(inputs["w_ff2"])[0],
        "c_ident": np.eye(128, dtype=np.float32),
        "c_oh": np.zeros((32, 512), np.float32),
        "c_cm": _chunk_mask(),
        "c_bn": _bias_layout(f(inputs["rel_bias"])),
    }
    maps = []
    for c in range(NCORES):
        m = dict(shared)
        for nm in ("w_in", "w_o_a", "w_o_b", "w_out", "w_ff1", "w_ff2"):
            a = shared[nm]
            m[nm] = np.concatenate([a, np.full((1, a.shape[1]), float(c), np.float32)], 0)
        m["xp"] = np.ascontiguousarray(xb[c::8].reshape(NPT * 128, D))
        m["xs"] = np.ascontiguousarray(x_sample[2 * c:2 * c + 2].reshape(NST * DEC, D))
        m["cak"] = np.ascontiguousarray(f(inputs["cache_a_k"])[0, 2 * c:2 * c + 2].reshape(NST, PAST, 1024))
        m["cav"] = np.ascontiguousarray(f(inputs["cache_a_v"])[0, 2 * c:2 * c + 2].reshape(NST, PAST, 1024))
        m["cbk"] = np.ascontiguousarray(f(inputs["cache_b_k"])[0, 2 * c:2 * c + 2].reshape(NST, PAST, 256))
        m["cbv"] = np.ascontiguousarray(f(inputs["cache_b_v"])[0, 2 * c:2 * c + 2].reshape(NST, PAST, 256))
        m["cbi"] = np.ascontiguousarray(f(inputs["cache_b_kidx"])[0, 2 * c:2 * c + 2].reshape(NST, PAST, 64))
        sel = np.zeros(27, np.float32)
        for r in range(8):
            sel[r] = 1.0 if r == c - 1 else 0.0
            sel[9 + r] = 1.0 if r == c else 0.0
            sel[18 + r] = 1.0 if r > c else 0.0
        sel[8] = 1.0 if c == 0 else 0.0
        m["c_sel"] = sel
        maps.append(m)
    return maps


def assemble(results):
    def prompt(name, w):
        out = np.zeros((128, 128, w), np.float32)
        for c in range(NCORES):
            out[c::8] = results[c][name].reshape(NPT, 128, w)
        return out.reshape(SEQ, w)

    def sample(name, w):
        return np.concatenate([results[c][name].reshape(NST, DEC, w) for c in range(NCORES)], 0)

    y_prompt = prompt("yp", D).reshape(1, SEQ, D)
    y_sample = sample("ys", D)
    p_a_k = prompt("pak", 1024).reshape(1, 1, SEQ, 8, 128)
    p_a_v = prompt("pav", 1024).reshape(1, 1, SEQ, 8, 128)
    p_b_k = prompt("pbk", 256).reshape(1, 1, SEQ, 2, 128)
    p_b_v = prompt("pbv", 256).reshape(1, 1, SEQ, 2, 128)
    p_b_i = prompt("pbi", 64).reshape(1, 1, SEQ, 64)
    s_a_k = sample("sak", 1024).reshape(1, 16, DEC, 8, 128)
    s_a_v = sample("sav", 1024).reshape(1, 16, DEC, 8, 128)
    s_b_k = sample("sbk", 256).reshape(1, 16, DEC, 2, 128)
    s_b_v = sample("sbv", 256).reshape(1, 16, DEC, 2, 128)
    s_b_i = sample("sbi", 64).reshape(1, 16, DEC, 64)
    return (y_prompt, y_sample, p_a_k, p_a_v, p_b_k, p_b_v, p_b_i, s_a_k, s_a_v, s_b_k, s_b_v, s_b_i)


_NC_CACHE = {}


def kernel(**inputs):
    if "nc" not in _NC_CACHE:
        import os
        _NC_CACHE["nc"] = build_program(stop_after=int(os.environ.get("KSTOP", "99")))
    nc = _NC_CACHE["nc"]
    in_maps = make_in_maps(inputs)
    res = run_bass_kernel_spmd(nc, in_maps, core_ids=list(range(NCORES)))
    return assemble(res.results)
```
